# Optimizing a Trainium2 kernel written in Bass

```python
import math
import jax, jax.numpy as jnp
from jax import lax
import numpy as np

D_MODEL = 2048
BATCH = 2
SEQ = 4096
DEPTH = 1

GRID_W = 64
CTX_LEN = 256
GLA_HEADS = 8
GLA_DK = 64
GLA_DV = 128
GLA_RANK = 16
GLA_TAU = 16.0
GLA_CHUNK = 64
DIFF_HEADS = 8
DIFF_DQK = 64
DIFF_DV = 128
D_FF = 5632
CONV_W = 3
Q_BLOCK = 128
ROPE_THETA = 10000.0
EPS = 1e-6

GLA_QK = GLA_HEADS * GLA_DK
GLA_V = GLA_HEADS * GLA_DV
DIFF_QK = DIFF_HEADS * 2 * DIFF_DQK
DIFF_V = DIFF_HEADS * DIFF_DV
IN_SIZES = [GLA_QK, GLA_QK, GLA_V, GLA_V, GLA_RANK, GLA_RANK, DIFF_QK, DIFF_QK, DIFF_V]
D_IN = sum(IN_SIZES)
IN_OFFSETS = [int(v) for v in np.cumsum(IN_SIZES)[:-1]]

kernel_name = "hybrid_gla_diffattn_convffn_dit_block"


def rms_norm(x, w):
    xf = x.astype(jnp.float32)
    y = xf * lax.rsqrt(jnp.mean(xf * xf, axis=-1, keepdims=True) + EPS)
    return (y * w.astype(jnp.float32)).astype(x.dtype)


def modulate(h, shift, scale):
    return h * (1.0 + scale) + shift


def to_heads(t, n, d):
    B, T, _ = t.shape
    return t.reshape(B, T, n, d).transpose(0, 2, 1, 3)


def from_heads(t):
    B, n, T, d = t.shape
    return t.transpose(0, 2, 1, 3).reshape(B, T, n * d)


def rope_axis(x, ang):
    half = x.shape[-1] // 2
    x1, x2 = x[..., :half], x[..., half:]
    cos = jnp.cos(ang).astype(x.dtype)
    sin = jnp.sin(ang).astype(x.dtype)
    return jnp.concatenate([x1 * cos - x2 * sin, x2 * cos + x1 * sin], axis=-1)


def rope_2d(x, ang_row, ang_col):
    r = x.shape[-1] // 2
    return jnp.concatenate([rope_axis(x[..., :r], ang_row), rope_axis(x[..., r:], ang_col)], axis=-1)


def gla_scan(q, k, v, log_a, s0):
    B, H, T, dk = q.shape
    dv = v.shape[-1]
    n = T // GLA_CHUNK

    def chunks(t):
        return jnp.moveaxis(t.astype(jnp.float32).reshape(B, H, n, GLA_CHUNK, t.shape[-1]), 2, 0)

    lower = jnp.tril(jnp.ones((GLA_CHUNK, GLA_CHUNK), dtype=bool))[:, :, None]

    def step(state, inp):
        qc, kc, vc, ac = inp
        b = jnp.cumsum(ac, axis=2)
        o_inter = jnp.einsum('bhid,bhde->bhie', qc * jnp.exp(b), state)
        rel = b[:, :, :, None, :] - b[:, :, None, :, :]
        decay = jnp.exp(jnp.where(lower, rel, -jnp.inf))
        scores = jnp.einsum('bhid,bhjd,bhijd->bhij', qc, kc, decay)
        o_intra = jnp.einsum('bhij,bhje->bhie', scores, vc)
        b_last = b[:, :, -1:, :]
        state = jnp.exp(b_last[:, :, 0, :])[..., None] * state + jnp.einsum(
            'bhjd,bhje->bhde', kc * jnp.exp(b_last - b), vc)
        return state, o_inter + o_intra

    s_final, o = lax.scan(step, s0.astype(jnp.float32), (chunks(q), chunks(k), chunks(v), chunks(log_a)))
    o = jnp.moveaxis(o, 0, 2).reshape(B, H, T, dv)
    return o.astype(v.dtype), s_final


def diff_softmax(q, k, v, lam):
    s = jnp.einsum('bhmqd,bhmkd->bhmqk', q, k).astype(jnp.float32) * (DIFF_DQK ** -0.5)
    p = jax.nn.softmax(s, axis=-1)
    a = p[:, :, 0] - lam * p[:, :, 1]
    return jnp.einsum('bhqk,bhkd->bhqd', a.astype(v.dtype), v)


def conv_ffn(h, w_up, conv_w, conv_b, w_down):
    u = h @ w_up
    T = u.shape[1]
    pad = CONV_W // 2
    up = jnp.pad(u, ((0, 0), (pad, pad), (0, 0)))
    u = sum(up[:, j:j + T] * conv_w[j] for j in range(CONV_W)) + conv_b
    gate, val = jnp.split(u, 2, axis=-1)
    return (jax.nn.silu(gate) * val) @ w_down


def hybrid_layer(x, ctx, c, c_ctx, w_ada, b_ada, norm1_w, w_in, w_a_up_f, b_a_f, w_a_up_b, b_a_b,
                 gla_onorm_w, diff_qnorm_w, diff_knorm_w, lambda_q1, lambda_k1, lambda_q2, lambda_k2,
                 diff_onorm_w, w_proj_gla, w_proj_diff, w_gate, b_gate, w_out, norm2_w, w_up, conv_w,
                 conv_b, w_down, ang_row, ang_col, lam_init, update_ctx):
    B, S, _ = x.shape
    sh1, sc1, g1, sh2, sc2, g2 = [m[:, None, :] for m in
                                  jnp.split(jax.nn.silu(c) @ w_ada + b_ada, 6, axis=-1)]
    sh1c, sc1c, g1c, sh2c, sc2c, g2c = jnp.split(jax.nn.silu(c_ctx) @ w_ada + b_ada, 6, axis=-1)

    h_l = modulate(rms_norm(x, norm1_w), sh1, sc1)
    h_c = modulate(rms_norm(ctx, norm1_w), sh1c, sc1c)
    gq_l, gk_l, gv_l, gr_l, gaf_l, gab_l, dq_l, dk_l, dv_l = jnp.split(h_l @ w_in, IN_OFFSETS, axis=-1)
    gq_c, gk_c, gv_c, gr_c, gaf_c, gab_c, dq_c, dk_c, dv_c = jnp.split(h_c @ w_in, IN_OFFSETS, axis=-1)

    def gla_heads(gq, gk, gv, gaf, gab):
        q = to_heads(gq, GLA_HEADS, GLA_DK) * (GLA_DK ** -0.5)
        k = to_heads(gk, GLA_HEADS, GLA_DK)
        v = to_heads(gv, GLA_HEADS, GLA_DV)
        la_f = to_heads(jax.nn.log_sigmoid((gaf @ w_a_up_f + b_a_f).astype(jnp.float32)) / GLA_TAU,
                        GLA_HEADS, GLA_DK)
        la_b = to_heads(jax.nn.log_sigmoid((gab @ w_a_up_b + b_a_b).astype(jnp.float32)) / GLA_TAU,
                        GLA_HEADS, GLA_DK)
        return q, k, v, la_f, la_b

    aq_l, ak_l, av_l, alf_l, alb_l = gla_heads(gq_l, gk_l, gv_l, gaf_l, gab_l)
    aq_c, ak_c, av_c, alf_c, alb_c = gla_heads(gq_c, gk_c, gv_c, gaf_c, gab_c)
    zeros = jnp.zeros((B, GLA_HEADS, GLA_DK, GLA_DV), jnp.float32)

    def flip(t):
        return jnp.flip(t, axis=2)

    o_cf, s_cf = gla_scan(aq_c, ak_c, av_c, alf_c, zeros)
    o_lf, _ = gla_scan(aq_l, ak_l, av_l, alf_l, s_cf)
    o_cb, s_cb = gla_scan(flip(aq_c), flip(ak_c), flip(av_c), flip(alb_c), zeros)
    o_lb, _ = gla_scan(flip(aq_l), flip(ak_l), flip(av_l), flip(alb_l), s_cb)

    def gla_out(o, r):
        return from_heads(rms_norm(o, gla_onorm_w)) * jax.nn.silu(r)

    def diff_qk(t, w):
        Bt, T, _ = t.shape
        t = t.reshape(Bt, T, DIFF_HEADS, 2, DIFF_DQK).transpose(0, 2, 3, 1, 4)
        return rms_norm(t, w)

    q_l = rope_2d(diff_qk(dq_l, diff_qnorm_w), ang_row, ang_col)
    k_l = rope_2d(diff_qk(dk_l, diff_knorm_w), ang_row, ang_col)
    k_c = diff_qk(dk_c, diff_knorm_w)
    v_l = to_heads(dv_l, DIFF_HEADS, DIFF_DV)
    v_c = to_heads(dv_c, DIFF_HEADS, DIFF_DV)
    lam = (jnp.exp(jnp.sum(lambda_q1.astype(jnp.float32) * lambda_k1.astype(jnp.float32)))
           - jnp.exp(jnp.sum(lambda_q2.astype(jnp.float32) * lambda_k2.astype(jnp.float32))) + lam_init)

    k_all = jnp.concatenate([k_c, k_l], axis=3)
    v_all = jnp.concatenate([v_c, v_l], axis=2)
    nb = S // Q_BLOCK
    q_blocks = jnp.moveaxis(q_l.reshape(B, DIFF_HEADS, 2, nb, Q_BLOCK, DIFF_DQK), 3, 0)
    o_diff = lax.map(lambda qb: diff_softmax(qb, k_all, v_all, lam), q_blocks)
    o_diff = jnp.moveaxis(o_diff, 0, 2).reshape(B, DIFF_HEADS, S, DIFF_DV)

    def diff_out(o):
        return from_heads(rms_norm(o, diff_onorm_w) * (1.0 - lam_init))

    def merge(h, ya, yb):
        g_a, g_b = jnp.split(jax.nn.sigmoid(h @ w_gate + b_gate), 2, axis=-1)
        return (g_a * (ya @ w_proj_gla) + g_b * (yb @ w_proj_diff)) @ w_out

    mix_l = merge(h_l, gla_out(o_lf + flip(o_lb), gr_l), diff_out(o_diff))
    x_new = x + g1 * mix_l
    h2_l = modulate(rms_norm(x_new, norm2_w), sh2, sc2)
    x_new = x_new + g2 * conv_ffn(h2_l, w_up, conv_w, conv_b, w_down)

    if update_ctx:
        q_c = diff_qk(dq_c, diff_qnorm_w)
        o_diff_c = diff_softmax(q_c, k_c, v_c, lam)
        mix_c = merge(h_c, gla_out(o_cf + flip(o_cb), gr_c), diff_out(o_diff_c))
        ctx = ctx + g1c * mix_c
        h2_c = modulate(rms_norm(ctx, norm2_w), sh2c, sc2c)
        ctx = ctx + g2c * conv_ffn(h2_c, w_up, conv_w, conv_b, w_down)
    return x_new, ctx


def setup_inputs(seed: int = 0) -> dict:
    key = jax.random.key(seed)
    ks = jax.random.split(key, 32)
    f32 = jnp.float32
    D = D_MODEL

    def nrm(k, shape, scale):
        return jax.random.normal(k, shape, f32) * scale

    def gain(k, shape):
        return 1.0 + 0.02 * jax.random.normal(k, shape, f32)

    return {
        "x": nrm(ks[0], (BATCH, SEQ, D), 1.0),
        "c": nrm(ks[1], (BATCH, D), 1.0),
        "ctx": nrm(ks[2], (BATCH, CTX_LEN, D), 1.0),
        "c_ctx": nrm(ks[3], (D,), 1.0),
        "w_ada": nrm(ks[4], (DEPTH, D, 6 * D), D ** -0.5),
        "b_ada": nrm(ks[5], (DEPTH, 6 * D), 0.01),
        "norm1_w": gain(ks[6], (DEPTH, D)),
        "w_in": nrm(ks[7], (DEPTH, D, D_IN), D ** -0.5),
        "w_a_up_f": nrm(ks[8], (DEPTH, GLA_RANK, GLA_QK), GLA_RANK ** -0.5),
        "b_a_f": 1.0 + nrm(ks[9], (DEPTH, GLA_QK), 0.1),
        "w_a_up_b": nrm(ks[10], (DEPTH, GLA_RANK, GLA_QK), GLA_RANK ** -0.5),
        "b_a_b": 1.0 + nrm(ks[11], (DEPTH, GLA_QK), 0.1),
        "gla_onorm_w": gain(ks[12], (DEPTH, GLA_DV)),
        "diff_qnorm_w": gain(ks[13], (DEPTH, DIFF_DQK)),
        "diff_knorm_w": gain(ks[14], (DEPTH, DIFF_DQK)),
        "lambda_q1": nrm(ks[15], (DEPTH, DIFF_DQK), 0.1),
        "lambda_k1": nrm(ks[16], (DEPTH, DIFF_DQK), 0.1),
        "lambda_q2": nrm(ks[17], (DEPTH, DIFF_DQK), 0.1),
        "lambda_k2": nrm(ks[18], (DEPTH, DIFF_DQK), 0.1),
        "diff_onorm_w": gain(ks[19], (DEPTH, DIFF_DV)),
        "w_proj_gla": nrm(ks[20], (DEPTH, GLA_V, D), GLA_V ** -0.5),
        "w_proj_diff": nrm(ks[21], (DEPTH, DIFF_V, D), DIFF_V ** -0.5),
        "w_gate": nrm(ks[22], (DEPTH, D, 2 * D), D ** -0.5),
        "b_gate": nrm(ks[23], (DEPTH, 2 * D), 0.01),
        "w_out": nrm(ks[24], (DEPTH, D, D), D ** -0.5),
        "norm2_w": gain(ks[25], (DEPTH, D)),
        "w_up": nrm(ks[26], (DEPTH, D, 2 * D_FF), D ** -0.5),
        "conv_w": nrm(ks[27], (DEPTH, CONV_W, 2 * D_FF), CONV_W ** -0.5),
        "conv_b": nrm(ks[28], (DEPTH, 2 * D_FF), 0.01),
        "w_down": nrm(ks[29], (DEPTH, D_FF, D), D_FF ** -0.5),
    }


def reference(x, c, ctx, c_ctx, w_ada, b_ada, norm1_w, w_in, w_a_up_f, b_a_f, w_a_up_b, b_a_b,
              gla_onorm_w, diff_qnorm_w, diff_knorm_w, lambda_q1, lambda_k1, lambda_q2, lambda_k2,
              diff_onorm_w, w_proj_gla, w_proj_diff, w_gate, b_gate, w_out, norm2_w, w_up, conv_w,
              conv_b, w_down):
    S = x.shape[1]
    ROWS = S // GRID_W
    pos_row = jnp.repeat(jnp.arange(ROWS), GRID_W).astype(jnp.float32)
    pos_col = jnp.tile(jnp.arange(GRID_W), ROWS).astype(jnp.float32)
    n_freq = DIFF_DQK // 4
    inv_freq = ROPE_THETA ** (-jnp.arange(n_freq, dtype=jnp.float32) / n_freq)
    ang_row = pos_row[:, None] * inv_freq
    ang_col = pos_col[:, None] * inv_freq
    for i in range(DEPTH):
        lam_init = 0.8 - 0.6 * math.exp(-0.3 * i)
        x, ctx = hybrid_layer(
            x, ctx, c, c_ctx, w_ada[i], b_ada[i], norm1_w[i], w_in[i], w_a_up_f[i], b_a_f[i],
            w_a_up_b[i], b_a_b[i], gla_onorm_w[i], diff_qnorm_w[i], diff_knorm_w[i], lambda_q1[i],
            lambda_k1[i], lambda_q2[i], lambda_k2[i], diff_onorm_w[i], w_proj_gla[i], w_proj_diff[i],
            w_gate[i], b_gate[i], w_out[i], norm2_w[i], w_up[i], conv_w[i], conv_b[i], w_down[i],
            ang_row, ang_col, lam_init, i < DEPTH - 1)
    return x
```

```python
import numpy as np
from contextlib import ExitStack
import concourse.bass as bass
import concourse.mybir as mybir
from concourse.bass_utils import run_bass_kernel_spmd

F32 = mybir.dt.float32
BF16 = mybir.dt.bfloat16
AF = mybir.ActivationFunctionType
ALU = mybir.AluOpType
AX = mybir.AxisListType

D = 2048
KC = 16
S = 4096
CTX = 256
TALL = 4608
NTALL = 36
TOWN = 1280
NTOWN = 10
DFF = 5632
NJ = 44
EPS = 1e-6
LAM_INIT = 0.8 - 0.6 * 1.0
ENGS = ['pe', 'act', 'dve', 'pool', 'sp']


class Sched:
    def __init__(self, nc, es):
        self.nc = nc
        self.ops = {e: [] for e in ENGS}
        self.res = {}
        self.sems = [es.enter_context(nc.semaphore(f"sm{i}")) for i in range(100)]
        self.nsem = 0
        self.prog = {}
        for e in ENGS:
            self.prog[e] = self.sems[self.nsem]
            self.nsem += 1
        self.slot = {}
        self.last = {}
        self.active = True

    def _slot(self, key):
        if key not in self.slot:
            self.slot[key] = [self.nsem, 0]
            self.nsem += 1
            assert self.nsem <= len(self.sems), "out of semaphores"
        return self.slot[key]

    def op(self, eng, fn, r=(), w=(), dma=None):
        if not self.active:
            return
        idx = len(self.ops[eng])
        if dma is not None:
            sl = self._slot(dma)
            sl[1] += 1
            ev = ('d', sl[0], 16 * sl[1])
            src = ('d', sl[0])
        else:
            ev = ('c', eng, idx)
            src = eng
        waits = []
        for k in r:
            st = self.res.setdefault(k, {'w': {}, 'r': {}})
            for s2, e2 in st['w'].items():
                if s2 == eng and eng == 'pe' and dma is None:
                    continue
                waits.append(e2)
        for k in w:
            st = self.res.setdefault(k, {'w': {}, 'r': {}})
            for dd in (st['w'], st['r']):
                for s2, e2 in dd.items():
                    if s2 == eng and dma is None:
                        continue
                    waits.append(e2)
        for k in r:
            self.res[k]['r'][src] = ev
        for k in w:
            self.res[k]['w'] = {src: ev}
            self.res[k]['r'] = {}
        self.last[src] = ev
        self.ops[eng].append({'fn': fn, 'waits': waits, 'dma': (self.sems[ev[1]] if dma is not None else None),
                              'ms': False})

    def barrier(self):
        if not self.active:
            return
        evs = list(self.last.values())
        for e in ENGS:
            self.ops[e].append({'fn': None, 'waits': list(evs), 'dma': None, 'ms': False})
        self.res = {}

    def emit(self):
        nc = self.nc
        for e in ENGS:
            for o in self.ops[e]:
                for ev in o['waits']:
                    if ev[0] == 'c':
                        self.ops[ev[1]][ev[2]]['ms'] = True
        rank = {}
        for e in ENGS:
            c = 0
            for i, o in enumerate(self.ops[e]):
                if o['ms']:
                    c += 1
                    rank[(e, i)] = c
        prog = self.prog
        sems = self.sems

        def run(ename, eng):
            known = {}
            for o in self.ops[ename]:
                need = {}
                for ev in o['waits']:
                    if ev[0] == 'c':
                        sid, val = id(prog[ev[1]]), rank[(ev[1], ev[2])]
                        sem = prog[ev[1]]
                    else:
                        sem = sems[ev[1]]
                        sid, val = id(sem), ev[2]
                    if sid not in need or need[sid][1] < val:
                        need[sid] = (sem, val)
                for sid, (sem, val) in need.items():
                    if known.get(sid, 0) < val:
                        eng.wait_ge(sem, val)
                        known[sid] = val
                if o['fn'] is not None:
                    ins = o['fn'](eng)
                    if o['dma'] is not None:
                        ins.then_inc(o['dma'], 16)
                    elif o['ms']:
                        ins.then_inc(prog[ename], 1)

        with nc.Block() as block:
            block.tensor(lambda e: run('pe', e))
            block.scalar(lambda e: run('act', e))
            block.vector(lambda e: run('dve', e))
            block.gpsimd(lambda e: run('pool', e))
            block.sync(lambda e: run('sp', e))


def build(stop_after=99, taps=(), start_at=0, feeds=()):
    nc = bass.Bass("TRN2", target_bir_lowering=False)
    es = ExitStack()
    with es:
        def din(name, shape, dt=F32):
            return nc.dram_tensor(name, list(shape), dt, kind="ExternalInput").ap()

        def dscr(name, shape, dt):
            kind = "ExternalOutput" if name in taps else ("ExternalInput" if name in feeds else "Internal")
            return nc.dram_tensor(name, list(shape), dt, kind=kind).ap()

        ISPEC = {
            'xall': [TALL, D],
            'xown': [TOWN, D],
            'cT': [128, KC, 2],
            'w_ada': [D, 6 * D],
            'b_ada2': [2, 6 * D],
            'nw1T': [128, KC],
            'nw2T': [128, KC],
            'w_in': [D, 6176],
            'waf': [33, 512],
            'wab': [33, 512],
            'ident': [128, 128],
            'trif': [128, 128],
            'trib': [128, 128],
            'mskf': [128, 128],
            'mskb': [128, 128],
            'mcols': [128, 3 * NTALL],
            'edge': [128, 2],
            'ropeC': [TALL, 512],
            'ropeS': [TALL, 512],
            'ropeCo': [TOWN, 512],
            'ropeSo': [TOWN, 512],
            'qnw': [128, 512],
            'knw': [128, 512],
            'gonw': [128, 1024],
            'donw': [128, 128],
            'lamv': [128, 4, 64],
            'w_pg': [1024, D],
            'w_pd': [1024, D],
            'w_gate': [D, 2 * D],
            'bgT': [128, 32],
            'w_out': [D, D],
            'w_up': [D, 2 * DFF],
            'cwT': [128, 88, 3],
            'cbT': [128, 88],
            'w_down': [DFF, D],
        }
        declared = {}

        def I(name):
            if name not in declared:
                declared[name] = din(name, ISPEC[name])
            return declared[name]
        PH_INPUTS = {6: ['w_up','w_down','cwT','cbT'], 5: ['w_gate','w_pg','w_pd','w_out','bgT','xown'], 4: ['donw','lamv'], 3: ['waf','wab','mskf','mskb','gonw'], 2: ['w_in','qnw','knw','ropeC','ropeS','ropeCo','ropeSo'], 1: ['cT','w_ada','b_ada2','xall','xown'], -1: ['ident','nw1T','nw2T','mcols','edge']}
        for ph, names in PH_INPUTS.items():
            if ph == -1 or start_at <= ph <= stop_after:
                for n_ in names:
                    I(n_)

        y = nc.dram_tensor("y", [1024, D], F32, kind="ExternalOutput").ap()

        modD = dscr("modD", [2, 6 * D], F32)
        hTall = dscr("hTall", [KC, 128, TALL], BF16)
        hTown = dscr("hTown", [KC, 128, TOWN], BF16)
        gkTa = dscr("gkTa", [4, 128, TALL], BF16)
        gva = dscr("gva", [TALL, 1024], BF16)
        gafTa = dscr("gafTa", [16, TALL], BF16)
        gabTa = dscr("gabTa", [16, TALL], BF16)
        dkT = dscr("dkT", [8, 128, TALL], BF16)
        dv1 = dscr("dv1", [TALL, 8, 129], BF16)
        gqTo = dscr("gqTo", [4, 128, TOWN], BF16)
        gkTo = dscr("gkTo", [4, 128, TOWN], BF16)
        gvo = dscr("gvo", [TOWN, 1024], BF16)
        gro = dscr("gro", [TOWN, 1024], BF16)
        gafTo = dscr("gafTo", [16, TOWN], BF16)
        gabTo = dscr("gabTo", [16, TOWN], BF16)
        dqT = dscr("dqT", [8, 128, TOWN], BF16)
        yaT = dscr("yaT", [8, 128, TOWN], BF16)
        ybT = dscr("ybT", [8, 128, TOWN], BF16)
        xnewD = dscr("xnewD", [TOWN, D], F32)

        sb = lambda name, shape, dt: es.enter_context(nc.sbuf_tensor(name, list(shape), dt))
        ps = lambda name, shape, dt: es.enter_context(nc.psum_tensor(name, list(shape), dt))
        sch = Sched(nc, es)
        op = sch.op
        _u = [0]

        def uniq():
            _u[0] += 1
            return f'dram{_u[0]}'

        ident = sb("ident_s", [128, 128], BF16)
        modT = sb("modT", [128, 2, 6, KC], F32)
        A1 = sb("A1", [128, 2, KC], F32)
        A2 = sb("A2", [128, KC], F32)
        nw1s = sb("nw1s", [128, KC], F32)
        nw2s = sb("nw2s", [128, KC], F32)
        mcols = sb("mcols_s", [128, 3 * NTALL], F32)
        edge = sb("edge_s", [128, 2], F32)
        stat = sb("stat", [128, 64], F32)

        XB = [sb(f"XB{i}", [128, KC, 512], BF16) for i in range(2)]
        WB = [sb(f"WB{i}", [128, KC, 512], BF16) for i in range(2)]
        T32 = [sb(f"T32_{i}", [128, 2048], F32) for i in range(2)]
        U32 = [sb(f"U32_{i}", [128, 2048], F32) for i in range(2)]
        T16 = [sb(f"T16_{i}", [128, 2048], BF16) for i in range(2)]
        SG = [sb(f"SG{i}", [128, 1040], BF16) for i in range(2)]
        PSB = [ps(f"PS{i}", [128, 512], F32) for i in range(8)]

        ARENA = sb("ARENA", [128, 48128], BF16)

        def av(off, shape, dt=BF16):
            n = int(np.prod(shape)) * (2 if dt is F32 else 1)
            ap = ARENA[:, off:off + n]
            if dt is F32:
                ap = ap.bitcast(F32)
            if len(shape) == 2:
                ap = ap.rearrange("p (a b) -> p a b", b=shape[1])
            elif len(shape) == 3:
                ap = ap.rearrange("p (a b c) -> p a b c", b=shape[1], c=shape[2])
            return ap

        def psbf(i):
            return PSB[i][:].bitcast(BF16)

        def phase(k):
            sch.active = (k is None) or (start_at <= k <= stop_after)

        phase(None)
        op('pool', lambda e: e.dma_start(out=ident[:], in_=I('ident')[:, :]), w=['ident'], dma='ident')
        op('sp', lambda e: e.dma_start(out=nw1s[:], in_=I('nw1T')[:, :]), w=['nw1s'], dma='nw1s')
        op('sp', lambda e: e.dma_start(out=nw2s[:], in_=I('nw2T')[:, :]), w=['nw2s'], dma='nw2s')
        op('sp', lambda e: e.dma_start(out=mcols[:], in_=I('mcols')[:, :]), w=['mcols'], dma='mcols')
        op('sp', lambda e: e.dma_start(out=edge[:], in_=I('edge')[:, :]), w=['edge'], dma='edge')
        epsc = sb("epsc", [128, 1], F32)
        op('dve', lambda e: e.memset(epsc[:], EPS), w=['epsc'])
        phase(1)
        cTs = T32[0][:, 0:32].rearrange("p (k b) -> p k b", b=2)
        scT = T16[0][:, 0:32].rearrange("p (k b) -> p k b", b=2)
        op('sp', lambda e: e.dma_start(out=cTs, in_=I('cT')[:, :, :]), w=['T32_0'], dma='T32_0')
        op('act', lambda e: e.activation(out=scT, in_=cTs, func=AF.Silu), r=['T32_0'], w=['T16_0'])
        bada = U32[0]
        for g in range(24):
            wslot = g % 2
            wk = f'WB{wslot}'
            op('pool', lambda e, g=g, wslot=wslot: e.dma_start(
                out=WB[wslot][:], in_=I('w_ada')[:, g * 512:(g + 1) * 512].rearrange("(k p) n -> p k n", p=128)),
                w=[wk], dma=wk)
            bk = f'U32_{g % 2}'
            op('sp', lambda e, g=g: e.dma_start(out=U32[g % 2][0:2, 0:512], in_=I('b_ada2')[:, g * 512:(g + 1) * 512]),
               w=[bk], dma=bk)
            pk = f'PS{g % 2}'
            for kc in range(KC):
                op('pe', lambda e, g=g, kc=kc, wslot=wslot: e.matmul(
                    PSB[g % 2][0:2, :], lhsT=scT[:, kc, :], rhs=WB[wslot][:, kc, :],
                    start=(kc == 0), stop=(kc == KC - 1)),
                    r=[wk, 'T16_0'], w=[pk])
            sk = f'T32_1s{g % 2}'
            op('dve', lambda e, g=g: e.tensor_tensor(
                out=T32[1][0:2, (g % 2) * 512:(g % 2) * 512 + 512], in0=PSB[g % 2][0:2, :],
                in1=U32[g % 2][0:2, 0:512], op=ALU.add), r=[pk, bk], w=[sk])
            op('sp', lambda e, g=g: e.dma_start(
                out=modD[:, g * 512:(g + 1) * 512], in_=T32[1][0:2, (g % 2) * 512:(g % 2) * 512 + 512]),
                r=[sk], w=[uniq()], dma=sk)
        sch.barrier()
        phase(None)
        for v in range(2):
            for s_ in range(6):
                op('sp', lambda e, v=v, s_=s_: e.dma_start(
                    out=modT[:, v, s_, :],
                    in_=modD[v:v + 1, s_ * D:(s_ + 1) * D].rearrange("o (k p) -> p (o k)", p=128),
                    allow_slow_non_contiguous=True), w=['modT'], dma=f'modT{v}{s_}')
        for v in range(2):
            op('dve', lambda e, v=v: e.scalar_tensor_tensor(
                out=A1[:, v, :], in0=modT[:, v, 1, :], scalar=1.0, in1=nw1s[:], op0=ALU.add, op1=ALU.mult),
                r=['modT', 'nw1s'], w=['A1'])
        op('dve', lambda e: e.scalar_tensor_tensor(
            out=A2[:], in0=modT[:, 0, 4, :], scalar=1.0, in1=nw2s[:], op0=ALU.add, op1=ALU.mult),
            r=['modT', 'nw2s'], w=['A2'])
        sch.barrier()

        def k1(src, ntiles, dst, A_of, B_of, tagp, dst_sb=None):
            nblk = (ntiles + 3) // 4
            for b in range(nblk):
                t0 = b * 4
                nt = min(4, ntiles - t0)
                xb = b % 2
                xbk = f'XB{xb}'
                for ti in range(nt):
                    t = t0 + ti
                    sl = t % 2
                    op('sp', lambda e, t=t, sl=sl: e.dma_start(out=T32[sl][:], in_=src[t * 128:(t + 1) * 128, :]),
                       w=[f'T32_{sl}'], dma=f'T32_{sl}')
                    op('act', lambda e, t=t, sl=sl: e.activation(
                        out=U32[sl][:], in_=T32[sl][:], func=AF.Square, accum_out=stat[:, sl:sl + 1]),
                        r=[f'T32_{sl}'], w=[f'U32_{sl}', f'st{sl}'])
                    op('act', lambda e, sl=sl: e.activation(
                        out=stat[:, 2 + sl:3 + sl], in_=stat[:, sl:sl + 1], func=AF.Sqrt, scale=1.0 / D, bias=epsc[:]),
                        r=[f'st{sl}', 'epsc'], w=[f'sd{sl}'])
                    op('dve', lambda e, sl=sl: e.reciprocal(out=stat[:, 4 + sl:5 + sl], in_=stat[:, 2 + sl:3 + sl]),
                       r=[f'sd{sl}'], w=[f'rs{sl}'])
                    op('act', lambda e, sl=sl: e.activation(
                        out=T16[sl][:], in_=T32[sl][:], func=AF.Copy, scale=stat[:, 4 + sl:5 + sl]),
                        r=[f'T32_{sl}', f'rs{sl}'], w=[f'T16_{sl}'])
                    for half in range(2):
                        bank = 2 * sl + half
                        for k8 in range(8):
                            kc = half * 8 + k8
                            op('pe', lambda e, sl=sl, kc=kc, bank=bank, k8=k8: e.transpose(
                                out=psbf(bank)[:, k8 * 128:(k8 + 1) * 128], in_=T16[sl][:, kc * 128:(kc + 1) * 128],
                                identity=ident[:]),
                                r=[f'T16_{sl}', 'ident'], w=[f'PS{bank}'])
                        for k8 in range(8):
                            kc = half * 8 + k8
                            op('dve', lambda e, t=t, ti=ti, xb=xb, kc=kc, bank=bank, k8=k8: e.tensor_scalar(
                                out=(XB[xb][:, kc, ti * 128:(ti + 1) * 128] if dst_sb is None
                                     else dst_sb[:, kc, t * 128:(t + 1) * 128]),
                                in0=psbf(bank)[:, k8 * 128:(k8 + 1) * 128],
                                scalar1=A_of(t)[:, kc:kc + 1], scalar2=B_of(t)[:, kc:kc + 1],
                                op0=ALU.mult, op1=ALU.add),
                                r=[f'PS{bank}', 'A1', 'A2', 'modT'], w=[xbk if dst_sb is None else tagp])
                if dst_sb is not None:
                    continue
                op('pool', lambda e, xb=xb, t0=t0, nt=nt: e.dma_start(
                    out=dst[:, :, t0 * 128:(t0 + nt) * 128].rearrange("k p t -> p k t"),
                    in_=XB[xb][:, :, 0:nt * 128]), r=[xbk], w=[uniq()], dma=xbk)

        phase(1)
        if start_at <= 1 <= stop_after:
            xall_d, xown_d = I('xall'), I('xown')
        else:
            xall_d = xown_d = None
        k1(xall_d, NTALL, hTall, lambda t: A1[:, (1 if t < 2 else 0), :], lambda t: modT[:, (1 if t < 2 else 0), 0, :],
           'hTall')
        k1(xown_d, NTOWN, hTown, lambda t: A1[:, 0, :], lambda t: modT[:, 0, 0, :], 'hTown')
        sch.barrier()


        phase(None)
        P2ON = start_at <= 2 <= stop_after
        cnt = {'w': 0, 'x': 0, 'b': 0, 's': 0}
        wqn = av(4096, [512], F32)
        wkn = av(5120, [512], F32)
        if P2ON:
            qnw_d, knw_d, w_in_d = I('qnw'), I('knw'), I('w_in')
            op('sp', lambda e: e.dma_start(out=wqn[:], in_=qnw_d[:, :]), w=['wqn'], dma='wqn')
            op('sp', lambda e: e.dma_start(out=wkn[:], in_=knw_d[:, :]), w=['wkn'], dma='wkn')
        RC = [av(i * 1024, [512], F32) for i in range(2)]
        RS = [av(2048 + i * 1024, [512], F32) for i in range(2)]
        onec = sb("onec", [128, 1], F32)
        op('dve', lambda e: e.memset(onec[:], 1.0), w=['onec'])
        op('dve', lambda e: e.memset(stat[:, 32:36], 0.0), w=['onec'])

        def gemm(XT, T, col0, ncols, mode, epi, M=128):
            ws = cnt['w'] % 2
            cnt['w'] += 1
            wk = f'WB{ws}'
            op('pool', lambda e: e.dma_start(
                out=WB[ws][:, :, 0:ncols],
                in_=w_in_d[:, col0:col0 + ncols].rearrange("(k p) n -> p k n", p=128)), w=[wk], dma=wk)
            nblk = (T + 511) // 512
            for blk in range(nblk):
                ntok = min(512, T - blk * 512)
                xs = cnt['x'] % 2
                cnt['x'] += 1
                xk = f'XB{xs}'
                op('sp', lambda e, blk=blk, ntok=ntok, xs=xs: e.dma_start(
                    out=XB[xs][:, :, 0:ntok],
                    in_=XT[:, :, blk * 512:blk * 512 + ntok].rearrange("k p t -> p k t")), w=[xk], dma=xk)
                if mode == 'tm':
                    for ti in range(ntok // 128):
                        bank = cnt['b'] % 4
                        cnt['b'] += 1
                        for kc in range(KC):
                            op('pe', lambda e, kc=kc, ti=ti, bank=bank, xs=xs: e.matmul(
                                PSB[bank][:, 0:ncols], lhsT=XB[xs][:, kc, ti * 128:(ti + 1) * 128],
                                rhs=WB[ws][:, kc, 0:ncols], start=(kc == 0), stop=(kc == KC - 1)),
                                r=[xk, wk], w=[f'PS{bank}'])
                        epi(blk * 4 + ti, bank)
                else:
                    for mc in range((ncols + M - 1) // M):
                        bank = cnt['b'] % 4
                        cnt['b'] += 1
                        for kc in range(KC):
                            op('pe', lambda e, kc=kc, mc=mc, bank=bank, xs=xs, ntok=ntok: e.matmul(
                                PSB[bank][0:M, 0:ntok], lhsT=WB[ws][:, kc, mc * M:(mc + 1) * M],
                                rhs=XB[xs][:, kc, 0:ntok], start=(kc == 0), stop=(kc == KC - 1)),
                                r=[xk, wk], w=[f'PS{bank}'])
                        epi(mc, blk, ntok, bank)

        def epi_fm(dst_of, M=128):
            def f(mc, blk, ntok, bank):
                s_ = cnt['s'] % 2
                cnt['s'] += 1
                op('act', lambda e: e.activation(out=SG[s_][0:M, 0:ntok], in_=PSB[bank][0:M, 0:ntok], func=AF.Copy),
                   r=[f'PS{bank}'], w=[f'SG{s_}'])
                op('act', lambda e: e.dma_start(out=dst_of(mc)[:, blk * 512:blk * 512 + ntok], in_=SG[s_][0:M, 0:ntok]),
                   r=[f'SG{s_}'], w=[uniq()], dma=f'SG{s_}')
            return f

        def epi_tm(dst, cofs):
            def f(tile, bank):
                s_ = cnt['s'] % 2
                cnt['s'] += 1
                op('act', lambda e: e.activation(out=SG[s_][:, 0:512], in_=PSB[bank][:, :], func=AF.Copy),
                   r=[f'PS{bank}'], w=[f'SG{s_}'])
                op('act', lambda e: e.dma_start(out=dst[tile * 128:(tile + 1) * 128, cofs:cofs + 512], in_=SG[s_][:, 0:512]),
                   r=[f'SG{s_}'], w=[uniq()], dma=f'SG{s_}')
            return f

        def epi_dv(g):
            def f(tile, bank):
                s_ = cnt['s'] % 2
                cnt['s'] += 1
                sg3 = SG[s_][:, 0:516].rearrange("p (h d) -> p h d", d=129)
                op('act', lambda e: e.activation(out=sg3[:, :, 0:128],
                                                 in_=PSB[bank][:, :].rearrange("p (h d) -> p h d", d=128), func=AF.Copy),
                   r=[f'PS{bank}'], w=[f'SG{s_}'])
                op('act', lambda e: e.activation(out=sg3[:, :, 128], in_=stat[:, 32:36], func=AF.Identity,
                                                 scale=0.0, bias=onec[:]),
                   r=['onec'], w=[f'SG{s_}'])
                op('act', lambda e: e.dma_start(out=dv1[tile * 128:(tile + 1) * 128, g * 4:(g + 1) * 4, :], in_=sg3),
                   r=[f'SG{s_}'], w=[uniq()], dma=f'SG{s_}')
            return f

        def epi_qk(g, wn, wnk, ropeCd, ropeSd, dst):
            def f(tile, bank):
                s_ = cnt['s'] % 2
                cnt['s'] += 1
                psb = PSB[bank]
                op('sp', lambda e: e.dma_start(out=RC[s_][:], in_=ropeCd[tile * 128:(tile + 1) * 128, :]),
                   w=[f'RC{s_}'], dma=f'RC{s_}')
                op('sp', lambda e: e.dma_start(out=RS[s_][:], in_=ropeSd[tile * 128:(tile + 1) * 128, :]),
                   w=[f'RS{s_}'], dma=f'RS{s_}')
                op('act', lambda e: e.activation(out=U32[s_][:, 0:512], in_=psb[:, :], func=AF.Square),
                   r=[f'PS{bank}'], w=[f'U32_{s_}'])
                op('dve', lambda e: e.tensor_reduce(
                    out=stat[:, 8:16], in_=U32[s_][:, 0:512].rearrange("p (g d) -> p g d", d=64), axis=AX.X, op=ALU.add),
                    r=[f'U32_{s_}'], w=['qst'])
                op('act', lambda e: e.activation(out=stat[:, 16:24], in_=stat[:, 8:16], func=AF.Sqrt, scale=1.0 / 64,
                                                 bias=epsc[:]), r=['qst', 'epsc'], w=['qsd'])
                op('dve', lambda e: e.reciprocal(out=stat[:, 24:32], in_=stat[:, 16:24]), r=['qsd'], w=['qrs'])
                for gg in range(8):
                    op('act', lambda e, gg=gg: e.activation(
                        out=T32[s_][:, gg * 64:(gg + 1) * 64], in_=psb[:, gg * 64:(gg + 1) * 64], func=AF.Copy,
                        scale=stat[:, 24 + gg:25 + gg]), r=[f'PS{bank}', 'qrs'], w=[f'T32_{s_}'])
                kn = T32[s_][:, 0:512]
                kw = T32[s_][:, 512:1024]
                t1 = U32[s_][:, 512:1024]
                t2 = U32[s_][:, 1024:1536]
                op('pool', lambda e: e.tensor_tensor(out=kw, in0=kn, in1=wn, op=ALU.mult),
                   r=[f'T32_{s_}', wnk], w=[f'T32w_{s_}'])
                op('pool', lambda e: e.tensor_tensor(out=t1, in0=kw, in1=RC[s_][:], op=ALU.mult),
                   r=[f'T32w_{s_}', f'RC{s_}', f'U32_{s_}'], w=[f'U32b_{s_}'])
                v5 = lambda ap: ap.rearrange("p (g h d) -> p g h d", h=2, d=16)
                for hh in range(2):
                    op('pool', lambda e, hh=hh: e.tensor_tensor(
                        out=v5(t2)[:, :, hh, :], in0=v5(kw)[:, :, 1 - hh, :], in1=v5(RS[s_][:])[:, :, hh, :], op=ALU.mult),
                        r=[f'T32w_{s_}', f'RS{s_}'], w=[f'U32b_{s_}'])
                op('pool', lambda e: e.tensor_tensor(out=T16[s_][:, 0:512], in0=t1, in1=t2, op=ALU.add),
                   r=[f'U32b_{s_}'], w=[f'T16_{s_}'])
                tb = 4 + s_
                for j in range(4):
                    op('pe', lambda e, j=j: e.transpose(out=psbf(tb)[:, j * 128:(j + 1) * 128],
                                                        in_=T16[s_][:, j * 128:(j + 1) * 128], identity=ident[:]),
                       r=[f'T16_{s_}', 'ident'], w=[f'PS{tb}'])
                op('act', lambda e: e.activation(out=SG[s_][:, 0:512], in_=psbf(tb)[:, 0:512], func=AF.Copy),
                   r=[f'PS{tb}'], w=[f'SG{s_}'])
                op('act', lambda e: e.dma_start(
                    out=dst[g * 4:(g + 1) * 4, :, tile * 128:(tile + 1) * 128].rearrange("h p t -> p h t"),
                    in_=SG[s_][:, 0:512].rearrange("p (h t) -> p h t", t=128)),
                   r=[f'SG{s_}'], w=[uniq()], dma=f'SG{s_}')
            return f

        phase(2)
        if P2ON:
            ropeC_d, ropeS_d, ropeCo_d, ropeSo_d = I('ropeC'), I('ropeS'), I('ropeCo'), I('ropeSo')
        else:
            ropeC_d = ropeS_d = ropeCo_d = ropeSo_d = None
        gemm(hTall, TALL, 512, 512, 'fm', epi_fm(lambda mc: gkTa[mc]))
        for g in range(2):
            gemm(hTall, TALL, 1024 + g * 512, 512, 'tm', epi_tm(gva, g * 512))
        gemm(hTall, TALL, 3072, 32, 'fm', epi_fm(lambda mc: (gafTa, gabTa)[mc], M=16), M=16)
        for g in range(2):
            gemm(hTall, TALL, 4128 + g * 512, 512, 'tm', epi_qk(g, wkn, 'wkn', ropeC_d, ropeS_d, dkT))
        for g in range(2):
            gemm(hTall, TALL, 5152 + g * 512, 512, 'tm', epi_dv(g))
        gemm(hTown, TOWN, 0, 512, 'fm', epi_fm(lambda mc: gqTo[mc]))
        gemm(hTown, TOWN, 512, 512, 'fm', epi_fm(lambda mc: gkTo[mc]))
        for g in range(2):
            gemm(hTown, TOWN, 1024 + g * 512, 512, 'tm', epi_tm(gvo, g * 512))
        for g in range(2):
            gemm(hTown, TOWN, 2048 + g * 512, 512, 'tm', epi_tm(gro, g * 512))
        gemm(hTown, TOWN, 3072, 32, 'fm', epi_fm(lambda mc: (gafTo, gabTo)[mc], M=16), M=16)
        for g in range(2):
            gemm(hTown, TOWN, 3104 + g * 512, 512, 'tm', epi_qk(g, wqn, 'wqn', ropeCo_d, ropeSo_d, dqT))
        sch.barrier()


        phase(3)
        P3ON = start_at <= 3 <= stop_after
        OF = av(0, [NTOWN, 1024], F32)
        gonw = av(20480, [1024], F32)
        MSK = {'f': av(22528, [128], F32), 'b': av(22784, [128], F32)}
        Sst = {'f': av(23040, [4, 128], F32), 'b': av(24064, [4, 128], F32)}
        Sbf = av(25088, [2, 4, 128])
        gkt_s = [[av(26112, [4, 128]), av(33536, [4, 128])], [av(35840, [4, 128]), av(36352, [4, 128])]]
        gqt_s = [[av(26624, [4, 128]), av(34048, [4, 128])], [av(36864, [4, 128]), av(37376, [4, 128])]]
        gvt_s = [[av(27136, [1024]), av(34560, [1024])], [av(37888, [1024]), av(38912, [1024])]]
        gcnt = [0, 0]
        WA = {'f': av(28160, [512]), 'b': av(28672, [512])}
        zA_s = {'f': [av(29184, [128]), av(35584, [128])], 'b': [av(29312, [128]), av(35712, [128])]}
        if P3ON:
            waf_d, wab_d, mskf_d, mskb_d, gonw_d = I('waf'), I('wab'), I('mskf'), I('mskb'), I('gonw')
        else:
            waf_d = wab_d = mskf_d = mskb_d = gonw_d = None
        op('sp', lambda e: e.dma_start(out=gonw[:], in_=gonw_d[:, :]), w=['gonw'], dma='gonw')
        for d, wd, md in (('f', waf_d, mskf_d), ('b', wab_d, mskb_d)):
            op('pool', lambda e, d=d, wd=wd: e.dma_start(out=WA[d][0:33, :], in_=wd[:, :]), w=[f'WA{d}'], dma=f'WA{d}')
            op('sp', lambda e, d=d, md=md: e.dma_start(out=MSK[d][:], in_=md[:, :]), w=[f'MSK{d}'], dma=f'MSK{d}')
            for z_ in range(2):
                op('dve', lambda e, d=d, z_=z_: e.memset(zA_s[d][z_][0:64, :], 0.0), w=[f'zA{d}{z_}'])
                op('dve', lambda e, d=d, z_=z_: e.memset(zA_s[d][z_][32:33, :], 1.0), w=[f'zA{d}{z_}'])
            op('dve', lambda e, d=d: e.memset(Sst[d][:], 0.0), w=[f'S{d}'])
        KZs = [av(29440, [2, 512]), av(39936, [2, 512])]
        for b_ in range(2):
            op('dve', lambda e, b_=b_: e.memset(KZs[b_].rearrange("p a b -> p (a b)"), 0.0), w=[f'ktok{b_}'])
        QZ = av(30464, [6, 512])
        op('dve', lambda e: e.memset(QZ[:].rearrange("p a b -> p (a b)"), 0.0), w=['QZ'])
        qz = [QZ[:, i, :].rearrange("p (f t) -> p f t", t=128) for i in range(6)]
        PT = SG[1][:, 0:1024].rearrange("p (h t) -> p h t", t=128)
        Gss = [stat[:, 40:48].rearrange("p (f c) -> p f c", c=2), stat[:, 48:56].rearrange("p (f c) -> p f c", c=2)]

        import os as _os
        CUT = int(_os.environ.get('P3CUT', '9'))

        def gla_tile(own, t, d, mi, outputs, bs=0):
            sl_ = gcnt[bs] % 2
            gcnt[bs] += 1
            gkt, gqt, gvt, zAd = gkt_s[bs][sl_], gqt_s[bs][sl_], gvt_s[bs][sl_], zA_s[d][sl_]
            kk, kq, kv, kz = f'gkt{bs}{sl_}', f'gqt{bs}{sl_}', f'gvt{bs}{sl_}', f'zA{d}{sl_}'
            B0 = 4 * bs
            KZ, Gs = KZs[bs], Gss[bs]
            ktT = T16[bs][:, 0:512].rearrange("p (f t) -> p f t", t=128)
            qtT = T16[bs][:, 512:1024].rearrange("p (f t) -> p f t", t=128)
            kP0, kP1 = f'PS{B0}', f'PS{B0 + 1}'
            gaT = (gafTo if d == 'f' else gabTo) if own else (gafTa if d == 'f' else gabTa)
            gkT_ = gkTo if own else gkTa
            gv_ = gvo if own else gva
            tc = slice(t * 128, (t + 1) * 128)
            op('sp', lambda e: e.dma_start(out=zAd[0:16, :], in_=gaT[:, tc]), w=[kz], dma=kz)
            op('sp', lambda e: e.dma_start(out=gkt[:], in_=gkT_[:, :, tc].rearrange("k p t -> p k t")), w=[kk], dma=kk)
            op('sp', lambda e: e.dma_start(out=gvt[:], in_=gv_[tc, :]), w=[kv], dma=kv)
            if outputs:
                op('sp', lambda e: e.dma_start(out=gqt[:], in_=gqTo[:, :, tc].rearrange("k p t -> p k t")),
                   w=[kq], dma=kq)
            op('pe', lambda e: e.matmul(PSB[B0][:, :], lhsT=zAd[0:33, :], rhs=WA[d][0:33, :], start=True, stop=True),
               r=[kz, f'WA{d}'], w=[kP0])
            op('act', lambda e: e.activation(out=U32[bs][:, 0:512], in_=PSB[B0][:, :], func=AF.Exp, scale=-1.0),
               r=[kP0], w=[f'e1{bs}'])
            op('act', lambda e: e.activation(out=T32[bs][:, 0:512], in_=U32[bs][:, 0:512], func=AF.Ln, bias=onec[:]),
               r=[f'e1{bs}', 'onec'], w=[f'L{bs}'])
            op('dve', lambda e: e.tensor_scalar(out=T32[bs][:, 512:1024], in0=T32[bs][:, 0:512], scalar1=mcols[:, mi:mi + 1],
                                                scalar2=None, op0=ALU.mult), r=[f'L{bs}', 'mcols'], w=[f'Lm{bs}'])
            for fc in range(4):
                op('pe', lambda e, fc=fc: e.matmul(PSB[B0][:, fc * 128:(fc + 1) * 128],
                                                   lhsT=T32[bs][:, 512 + fc * 128:512 + (fc + 1) * 128], rhs=MSK[d][:],
                                                   start=True, stop=True), r=[f'Lm{bs}', f'MSK{d}'], w=[kP0])
            op('act', lambda e: e.activation(out=U32[bs][:, 512:1024], in_=PSB[B0][:, :], func=AF.Exp, scale=1.0 / 16),
               r=[kP0], w=[f'Em{bs}'])
            g0 = 63 if d == 'f' else 0
            op('act', lambda e: e.activation(
                out=Gs, in_=PSB[B0][:, :].rearrange("p (f c x) -> p f c x", c=2, x=64)[:, :, :, g0],
                func=AF.Exp, scale=-1.0 / 16), r=[kP0], w=[f'Gs{bs}'])
            if outputs:
                op('act', lambda e: e.activation(out=U32[bs][:, 1024:1536], in_=PSB[B0][:, :], func=AF.Exp, scale=-1.0 / 16),
                   r=[kP0], w=[f'Ep{bs}'])
            op('dve', lambda e: e.tensor_tensor(out=T16[bs][:, 0:512], in0=gkt[:].rearrange("p f t -> p (f t)"),
                                                in1=U32[bs][:, 512:1024], op=ALU.mult), r=[kk, f'Em{bs}'], w=[f'ktT{bs}'])
            if outputs:
                op('dve', lambda e: e.scalar_tensor_tensor(
                    out=T16[bs][:, 512:1024], in0=gqt[:].rearrange("p f t -> p (f t)"), scalar=0.125,
                    in1=U32[bs][:, 1024:1536], op0=ALU.mult, op1=ALU.mult), r=[kq, f'Ep{bs}'], w=[f'qtT{bs}'])
                for p_ in range(2):
                    ps_ = slice(p_ * 64, p_ * 64 + 64)
                    op('pool', lambda e, p_=p_, ps_=ps_: e.tensor_copy(out=qz[4 + p_][ps_, :, :], in_=qtT[ps_, :, :]),
                       r=[f'qtT{bs}'], w=['QZ'])
                    for c in range(2):
                        op('pool', lambda e, c=c, p_=p_, ps_=ps_: e.tensor_copy(
                            out=qz[2 * c + p_][ps_, :, c * 64:(c + 1) * 64], in_=qtT[ps_, :, c * 64:(c + 1) * 64]),
                            r=[f'qtT{bs}'], w=['QZ'])
            for fc in range(4):
                op('pe', lambda e, fc=fc: e.transpose(out=psbf(B0 + 1)[:, fc * 128:(fc + 1) * 128], in_=ktT[:, fc, :],
                                                      identity=ident[:]), r=[f'ktT{bs}', 'ident'], w=[kP1])
            for c in range(2):
                cs = slice(c * 64, c * 64 + 64)
                op('dve', lambda e, c=c, cs=cs: e.tensor_scalar(out=KZ[cs, c, :], in0=psbf(B0 + 1)[cs, 0:512],
                                                                scalar1=mcols[cs, mi:mi + 1], scalar2=None, op0=ALU.mult),
                   r=[kP1, 'mcols'], w=[f'ktok{bs}'])
            P3V = int(_os.environ.get('P3V', '0'))
            if outputs and CUT >= 3:
                for h in (range(0, 8, 2) if P3V == 2 else range(8)):
                    fc, hb = h // 2, (h % 2) * 64
                    op('pe', lambda e, h=h, fc=fc, hb=hb: e.matmul(
                        PSB[4 + h // 4][:, (h % 4) * 128:(h % 4 + 1) * 128], lhsT=ktT[:, fc, :],
                        rhs=qz[4 + h % 2][:, fc, :], start=True, stop=True), r=[f'ktT{bs}', 'QZ'], w=[f'PS{4 + h // 4}'])
                for h in (range(0) if P3V in (1, 2) else range(8)):
                    op('dve', lambda e, h=h: e.tensor_tensor(
                        out=PT[:, h, :], in0=PSB[4 + h // 4][:, (h % 4) * 128:(h % 4 + 1) * 128], in1=MSK[d][:],
                        op=ALU.mult), r=[f'PS{4 + h // 4}', f'MSK{d}'], w=['PT'])
            corder = (0, 1) if d == 'f' else (1, 0)

            def chunk_step(c):
                for fc in range(4):
                    op('pe', lambda e, fc=fc, c=c: e.matmul(
                        PSB[B0 + 2 + fc // 2][:, (fc % 2) * 256:(fc % 2 + 1) * 256],
                        lhsT=KZ[:, c, fc * 128:(fc + 1) * 128],
                        rhs=gvt[:, fc * 256:(fc + 1) * 256], start=True, stop=True),
                        r=[f'ktok{bs}', kv], w=[f'PS{B0 + 2 + fc // 2}'])
                for bnk in range(2):
                    for half in range(2):
                        hs = slice(half * 64, half * 64 + 64)
                        op('dve', lambda e, bnk=bnk, half=half, hs=hs: e.tensor_tensor(
                            out=Sst[d][hs, 2 * bnk:2 * bnk + 2, :], in0=Sst[d][hs, 2 * bnk:2 * bnk + 2, :],
                            in1=PSB[B0 + 2 + bnk][hs, :].rearrange("p (f x) -> p f x", x=256)[:, :, half * 128:half * 128 + 128],
                            op=ALU.add), r=[f'PS{B0 + 2 + bnk}', f'S{d}'], w=[f'S{d}'])
                for fc in range(4):
                    op('dve', lambda e, fc=fc, c=c: e.tensor_scalar(
                        out=Sst[d][:, fc, :], in0=Sst[d][:, fc, :], scalar1=Gs[:, fc, c:c + 1], scalar2=None,
                        op0=ALU.mult), r=[f'S{d}', f'Gs{bs}'], w=[f'S{d}'])

            def snap(i):
                op('act', lambda e: e.activation(out=Sbf[:, i, :, :].rearrange("p f x -> p (f x)"),
                                                 in_=Sst[d][:].rearrange("p f x -> p (f x)"), func=AF.Copy),
                   r=[f'S{d}'], w=[f'Sbf{i}'])

            if outputs:
                snap(0)
            chunk_step(corder[0])
            if outputs:
                snap(1)
            chunk_step(corder[1])
            if outputs:
                for h in range(8):
                    fc = h // 2
                    ob_ = PSB[6 + h // 4][:, (h % 4) * 128:(h % 4 + 1) * 128]
                    op('pe', lambda e, h=h, ob_=ob_: e.matmul(ob_, lhsT=PT[:, h, :], rhs=gvt[:, h * 128:(h + 1) * 128],
                                                              start=True, stop=False), r=['PT', kv], w=[f'PS{6 + h // 4}'])
                    for ci, c in enumerate(corder):
                        op('pe', lambda e, h=h, fc=fc, c=c, ci=ci, ob_=ob_: e.matmul(
                            ob_, lhsT=qz[2 * c + h % 2][:, fc, :], rhs=Sbf[:, ci, fc, :], start=False, stop=(ci == 1)),
                            r=['QZ', f'Sbf{ci}'], w=[f'PS{6 + h // 4}'])
            if outputs and CUT >= 6:
                for bnk in range(2):
                    dst_ = OF[:, t, bnk * 512:(bnk + 1) * 512]
                    if d == 'f':
                        op('act', lambda e, bnk=bnk, dst_=dst_: e.activation(out=dst_, in_=PSB[6 + bnk][:, :], func=AF.Copy),
                           r=[f'PS{6 + bnk}'], w=[f'OF{t}'])
                    else:
                        op('dve', lambda e, bnk=bnk, dst_=dst_: e.tensor_tensor(out=dst_, in0=dst_, in1=PSB[6 + bnk][:, :],
                                                                                op=ALU.add),
                           r=[f'PS{6 + bnk}', f'OF{t}'], w=[f'OF{t}'])

        def gla_finish(t):
            tc = slice(t * 128, (t + 1) * 128)
            rt = XB[0][:, 2:4, :].rearrange("p a b -> p (a b)")
            yab = XB[0][:, 0:2, :].rearrange("p a b -> p (a b)")
            yat = XB[1][:, 0:2, :].rearrange("p a b -> p (a b)")
            op('sp', lambda e: e.dma_start(out=rt, in_=gro[tc, :]), w=['rt'], dma='rt')
            op('act', lambda e: e.activation(out=U32[1][:, 0:1024], in_=OF[:, t, :], func=AF.Square),
               r=[f'OF{t}'], w=['osq'])
            op('dve', lambda e: e.tensor_reduce(out=stat[:, 48:56],
                                                in_=U32[1][:, 0:1024].rearrange("p (h x) -> p h x", x=128),
                                                axis=AX.X, op=ALU.add), r=['osq'], w=['ost'])
            op('act', lambda e: e.activation(out=stat[:, 56:64], in_=stat[:, 48:56], func=AF.Sqrt, scale=1.0 / 128,
                                             bias=epsc[:]), r=['ost', 'epsc'], w=['osd'])
            op('dve', lambda e: e.reciprocal(out=stat[:, 48:56], in_=stat[:, 56:64]), r=['osd'], w=['ors'])
            op('act', lambda e: e.activation(out=T32[1][:, 0:1024], in_=rt, func=AF.Silu), r=['rt'], w=['sr'])
            for h in range(8):
                hs = slice(h * 128, (h + 1) * 128)
                op('dve', lambda e, h=h, hs=hs: e.scalar_tensor_tensor(
                    out=U32[1][:, 1024 + h * 128:1024 + (h + 1) * 128], in0=OF[:, t, hs], scalar=stat[:, 48 + h:49 + h],
                    in1=gonw[:, hs], op0=ALU.mult, op1=ALU.mult), r=[f'OF{t}', 'ors', 'gonw'], w=['yaf'])
            op('dve', lambda e: e.tensor_tensor(out=yab, in0=U32[1][:, 1024:2048], in1=T32[1][:, 0:1024], op=ALU.mult),
               r=['yaf', 'sr'], w=['yab'])
            for h in range(8):
                op('pe', lambda e, h=h: e.transpose(out=psbf(1)[:, h * 128:(h + 1) * 128], in_=yab[:, h * 128:(h + 1) * 128],
                                                    identity=ident[:]), r=['yab', 'ident'], w=['PS1'])
            op('act', lambda e: e.activation(out=yat, in_=psbf(1)[:, :], func=AF.Copy), r=['PS1'], w=['yat'])
            op('act', lambda e: e.dma_start(out=yaT[:, :, tc].rearrange("h p t -> p h t"),
                                           in_=yat.rearrange("p (h t) -> p h t", t=128)), r=['yat'], w=[uniq()], dma='yat')

        import os
        DBG = int(os.environ.get("P3DBG", "0"))
        if DBG == 1:
            gla_tile(False, 0, 'f', 0, False)
        elif DBG == 2:
            gla_tile(True, 0, 'f', 2 * NTALL, True)
        elif DBG == 3:
            gla_tile(True, 0, 'f', 2 * NTALL, True)
            gla_tile(True, 0, 'b', 2 * NTALL, True)
            gla_finish(0)
        else:
            border = [1, 0] + list(range(NTALL - 1, 1, -1))
            for i_ in range(NTALL):
                gla_tile(False, i_, 'f', i_, False, bs=0)
                gla_tile(False, border[i_], 'b', NTALL + border[i_], False, bs=1)
            for t in range(NTOWN):
                gla_tile(True, t, 'f', 2 * NTALL + t, True)
            for t in range(NTOWN - 1, -1, -1):
                gla_tile(True, t, 'b', 2 * NTALL + t, True)
                gla_finish(t)
        sch.barrier()

        phase(4)
        P4ON = start_at <= 4 <= stop_after
        KTh = [av(0, [4608]), av(4608, [4608])]
        V1h = [av(9216, [36, 129]), av(13860, [36, 129])]
        Qz = [av(18504, [2, 1280]), av(21064, [2, 1280])]
        od = av(23624, [10, 128], F32)
        ybb = av(26184, [1280])
        ybt = av(27464, [1280])
        donw = av(28744, [128], F32)
        lamv = av(29000, [4, 64], F32)
        epsd = sb("epsd", [128, 1], F32)
        if P4ON:
            donw_d, lamv_d = I('donw'), I('lamv')
        else:
            donw_d = lamv_d = None
        op('dve', lambda e: e.memset(epsd[:], EPS / (1.0 - LAM_INIT) ** 2), w=['epsd'])
        op('sp', lambda e: e.dma_start(out=donw, in_=donw_d[:, :]), w=['donw'], dma='donw')
        op('sp', lambda e: e.dma_start(out=lamv, in_=lamv_d[:, :, :]), w=['lamv'], dma='lamv')
        for s_ in range(2):
            op('dve', lambda e, s_=s_: e.memset(Qz[s_].rearrange("p m t -> p (m t)"), 0.0), w=[f'Qz{s_}'])
        for i_ in range(2):
            op('dve', lambda e, i_=i_: e.tensor_tensor(out=U32[1][:, i_ * 64:(i_ + 1) * 64], in0=lamv[:, 2 * i_, :],
                                                      in1=lamv[:, 2 * i_ + 1, :], op=ALU.mult), r=['lamv'], w=['lamp'])
            op('dve', lambda e, i_=i_: e.tensor_reduce(out=stat[:, 36 + i_:37 + i_], in_=U32[1][:, i_ * 64:(i_ + 1) * 64],
                                                      axis=AX.X, op=ALU.add), r=['lamp'], w=['lams'])
        op('act', lambda e: e.activation(out=stat[:, 60:62], in_=stat[:, 36:38], func=AF.Exp), r=['lams'], w=['lame'])
        op('dve', lambda e: e.tensor_tensor(out=stat[:, 38:39], in0=stat[:, 60:61], in1=stat[:, 61:62], op=ALU.subtract),
           r=['lame'], w=['lamd'])
        op('dve', lambda e: e.tensor_scalar(out=stat[:, 39:40], in0=stat[:, 38:39], scalar1=-1.0, scalar2=-LAM_INIT,
                                            op0=ALU.mult, op1=ALU.add), r=['lamd'], w=['neglam'])
        kts = [0, 1] + list(range(3, 35))
        for h in range(8):
            s_ = h % 2
            op('sp', lambda e, h=h, s_=s_: e.dma_start(out=KTh[s_], in_=dkT[h]), w=[f'KTh{s_}'], dma=f'KTh{s_}')
            op('sp', lambda e, h=h, s_=s_: e.dma_start(out=V1h[s_], in_=dv1[:, h, :].rearrange("(n p) d -> p n d", p=128)),
               w=[f'V1h{s_}'], dma=f'V1h{s_}')
            for m in range(2):
                ms = slice(m * 64, m * 64 + 64)
                op('sp', lambda e, h=h, s_=s_, m=m, ms=ms: e.dma_start(out=Qz[s_][ms, m, :], in_=dqT[h][ms, :]),
                   w=[f'Qz{s_}'], dma=f'Qz{s_}{m}')
            for (q0, nq) in ((0, 512), (512, 512), (1024, 256)):
                nqt = nq // 128
                for m in range(2):
                    def qk_exp(ki):
                        kt = kts[ki]
                        sbk = ki % 2
                        op('pe', lambda e, s_=s_, kt=kt, m=m, q0=q0, nq=nq, sbk=sbk: e.matmul(
                            PSB[sbk][:, 0:nq], lhsT=KTh[s_][:, kt * 128:(kt + 1) * 128], rhs=Qz[s_][:, m, q0:q0 + nq],
                            start=True, stop=True), r=[f'KTh{s_}', f'Qz{s_}'], w=[f'PS{sbk}'])
                        op('act', lambda e, nq=nq, sbk=sbk: e.activation(out=SG[sbk][:, 0:nq], in_=PSB[sbk][:, 0:nq],
                                                                        func=AF.Exp, scale=0.125),
                           r=[f'PS{sbk}'], w=[f'SG{sbk}'])

                    qk_exp(0)
                    for ki, kt in enumerate(kts):
                        sbk = ki % 2
                        if ki + 1 < len(kts):
                            qk_exp(ki + 1)
                        for qi in range(nqt):
                            op('pe', lambda e, s_=s_, kt=kt, qi=qi, sbk=sbk, ki=ki: e.matmul(
                                PSB[2 + qi][:, 0:129], lhsT=SG[sbk][:, qi * 128:(qi + 1) * 128], rhs=V1h[s_][:, kt, :],
                                start=(ki == 0), stop=(ki == len(kts) - 1)),
                                r=[f'SG{sbk}', f'V1h{s_}'], w=[f'PS{2 + qi}'])
                    for qi in range(nqt):
                        qt = q0 // 128 + qi
                        O_ = PSB[2 + qi]
                        op('dve', lambda e, qi=qi, O_=O_: e.reciprocal(out=stat[:, qi:qi + 1], in_=O_[:, 128:129]),
                           r=[f'PS{2 + qi}'], w=[f'rz{qi}'])
                        if m == 0:
                            op('dve', lambda e, qi=qi, qt=qt, O_=O_: e.tensor_scalar(
                                out=od[:, qt, :], in0=O_[:, 0:128], scalar1=stat[:, qi:qi + 1], scalar2=None, op0=ALU.mult),
                                r=[f'PS{2 + qi}', f'rz{qi}'], w=['od'])
                        else:
                            op('dve', lambda e, qi=qi: e.tensor_tensor(out=stat[:, 4 + qi:5 + qi], in0=stat[:, qi:qi + 1],
                                                                      in1=stat[:, 39:40], op=ALU.mult),
                               r=[f'rz{qi}', 'neglam'], w=[f'nrz{qi}'])
                            op('dve', lambda e, qi=qi, qt=qt, O_=O_: e.scalar_tensor_tensor(
                                out=od[:, qt, :], in0=O_[:, 0:128], scalar=stat[:, 4 + qi:5 + qi], in1=od[:, qt, :],
                                op0=ALU.mult, op1=ALU.add), r=[f'PS{2 + qi}', f'nrz{qi}', 'od'], w=['od'])
            odf = od.rearrange("p a b -> p (a b)")
            op('act', lambda e: e.activation(out=U32[0][:, 0:1280], in_=odf, func=AF.Square), r=['od'], w=['odsq'])
            op('dve', lambda e: e.tensor_reduce(out=stat[:, 8:18], in_=U32[0][:, 0:1280].rearrange("p (a b) -> p a b", b=128),
                                                axis=AX.X, op=ALU.add), r=['odsq'], w=['odst'])
            op('act', lambda e: e.activation(out=stat[:, 18:28], in_=stat[:, 8:18], func=AF.Sqrt,
                                             scale=1.0 / (128 * (1.0 - LAM_INIT) ** 2), bias=epsd[:]),
               r=['odst', 'epsd'], w=['odsd'])
            op('dve', lambda e: e.reciprocal(out=stat[:, 8:18], in_=stat[:, 18:28]), r=['odsd'], w=['odrs'])
            for qt in range(10):
                op('dve', lambda e, qt=qt: e.scalar_tensor_tensor(
                    out=ybb[:, qt * 128:(qt + 1) * 128], in0=od[:, qt, :], scalar=stat[:, 8 + qt:9 + qt], in1=donw,
                    op0=ALU.mult, op1=ALU.mult), r=['od', 'odrs', 'donw'], w=['ybb'])
            for qt in range(10):
                op('pe', lambda e, qt=qt: e.transpose(out=psbf(6 + qt // 8)[:, (qt % 8) * 128:(qt % 8 + 1) * 128],
                                                     in_=ybb[:, qt * 128:(qt + 1) * 128], identity=ident[:]),
                   r=['ybb', 'ident'], w=[f'PS{6 + qt // 8}'])
            op('act', lambda e: e.activation(out=ybt[:, 0:1024], in_=psbf(6)[:, :], func=AF.Copy), r=['PS6'], w=['ybt'])
            op('act', lambda e: e.activation(out=ybt[:, 1024:1280], in_=psbf(7)[:, 0:256], func=AF.Copy),
               r=['PS7'], w=['ybt'])
            op('act', lambda e, h=h: e.dma_start(out=ybT[h], in_=ybt), r=['ybt'], w=[uniq()], dma='ybt')
        sch.barrier()

        phase(5)
        P5ON = start_at <= 5 <= stop_after
        mergedT = av(0, [16, 1280])
        G1B = av(20480, [2048], F32)
        bg = av(24576, [32], F32)
        grow = av(24640, [2048], F32)
        ones_s = sb("ones_s", [128, 128], F32)
        if P5ON:
            w_gate_d, w_pg_d, w_pd_d, w_out_d, bgT_d, xown5_d = I('w_gate'), I('w_pg'), I('w_pd'), I('w_out'), I('bgT'), I('xown')
        else:
            w_gate_d = w_pg_d = w_pd_d = w_out_d = bgT_d = xown5_d = None
        phase(None)
        op('dve', lambda e: e.memset(ones_s[:], 1.0), w=['ones_s'])
        phase(5)

        def bcast_row(srcrow, dstB, key):
            op('sp', lambda e: e.dma_start(out=grow[0:1, :], in_=srcrow), w=['grow'], dma='grow')
            for c4 in range(4):
                op('pe', lambda e, c4=c4: e.matmul(PSB[4 + c4 % 2][:, :], lhsT=ones_s[0:1, :],
                                                   rhs=grow[0:1, c4 * 512:(c4 + 1) * 512], start=True, stop=True),
                   r=['ones_s', 'grow'], w=[f'PS{4 + c4 % 2}'])
                op('act', lambda e, c4=c4: e.activation(out=dstB[:, c4 * 512:(c4 + 1) * 512], in_=PSB[4 + c4 % 2][:, :],
                                                        func=AF.Copy), r=[f'PS{4 + c4 % 2}'], w=[key])

        bcast_row(modD[0:1, 2 * D:3 * D], G1B, 'G1B')
        op('sp', lambda e: e.dma_start(out=bg, in_=bgT_d[:, :]), w=['bg'], dma='bg')
        wc = 0
        for (q0, nq) in ((0, 512), (512, 512), (1024, 256)):
            op('sp', lambda e, q0=q0, nq=nq: e.dma_start(out=XB[0][:, :, 0:nq],
                                                         in_=hTown[:, :, q0:q0 + nq].rearrange("k p t -> p k t")),
               w=['XB0'], dma='XB0')
            op('sp', lambda e, q0=q0, nq=nq: e.dma_start(out=XB[1][:, 0:8, 0:nq],
                                                         in_=yaT[:, :, q0:q0 + nq].rearrange("k p t -> p k t")),
               w=['XB1a'], dma='XB1a')
            op('sp', lambda e, q0=q0, nq=nq: e.dma_start(out=XB[1][:, 8:16, 0:nq],
                                                         in_=ybT[:, :, q0:q0 + nq].rearrange("k p t -> p k t")),
               w=['XB1b'], dma='XB1b')
            for fo in range(16):
                ws = wc % 2
                wc += 1
                fs = slice(fo * 128, (fo + 1) * 128)
                op('pool', lambda e, ws=ws, fs=fs: e.dma_start(
                    out=WB[ws][:, :, 0:128], in_=w_gate_d[:, fs].rearrange("(k p) n -> p k n", p=128)),
                    w=[f'WB{ws}a'], dma=f'WB{ws}a')
                op('pool', lambda e, ws=ws, fo=fo: e.dma_start(
                    out=WB[ws][:, :, 128:256],
                    in_=w_gate_d[:, D + fo * 128:D + (fo + 1) * 128].rearrange("(k p) n -> p k n", p=128)),
                    w=[f'WB{ws}b'], dma=f'WB{ws}b')
                op('pool', lambda e, ws=ws, fs=fs: e.dma_start(
                    out=WB[ws][:, 0:8, 256:384], in_=w_pg_d[:, fs].rearrange("(k p) n -> p k n", p=128)),
                    w=[f'WB{ws}c'], dma=f'WB{ws}c')
                op('pool', lambda e, ws=ws, fs=fs: e.dma_start(
                    out=WB[ws][:, 0:8, 384:512], in_=w_pd_d[:, fs].rearrange("(k p) n -> p k n", p=128)),
                    w=[f'WB{ws}d'], dma=f'WB{ws}d')
                b0 = 4 * (fo % 2)
                for kc in range(KC):
                    op('pe', lambda e, ws=ws, kc=kc, nq=nq, b0=b0: e.matmul(
                        PSB[b0][:, 0:nq], lhsT=WB[ws][:, kc, 0:128], rhs=XB[0][:, kc, 0:nq],
                        start=(kc == 0), stop=(kc == KC - 1)), r=[f'WB{ws}a', 'XB0'], w=[f'PS{b0}'])
                for kc in range(KC):
                    op('pe', lambda e, ws=ws, kc=kc, nq=nq, b0=b0: e.matmul(
                        PSB[b0 + 1][:, 0:nq], lhsT=WB[ws][:, kc, 128:256], rhs=XB[0][:, kc, 0:nq],
                        start=(kc == 0), stop=(kc == KC - 1)), r=[f'WB{ws}b', 'XB0'], w=[f'PS{b0 + 1}'])
                for kc in range(8):
                    op('pe', lambda e, ws=ws, kc=kc, nq=nq, b0=b0: e.matmul(
                        PSB[b0 + 2][:, 0:nq], lhsT=WB[ws][:, kc, 256:384], rhs=XB[1][:, kc, 0:nq],
                        start=(kc == 0), stop=(kc == 7)), r=[f'WB{ws}c', 'XB1a'], w=[f'PS{b0 + 2}'])
                for kc in range(8):
                    op('pe', lambda e, ws=ws, kc=kc, nq=nq, b0=b0: e.matmul(
                        PSB[b0 + 3][:, 0:nq], lhsT=WB[ws][:, kc, 384:512], rhs=XB[1][:, 8 + kc, 0:nq],
                        start=(kc == 0), stop=(kc == 7)), r=[f'WB{ws}d', 'XB1b'], w=[f'PS{b0 + 3}'])
                for i_ in range(2):
                    op('act', lambda e, i_=i_, fo=fo, nq=nq, b0=b0: e.activation(
                        out=T32[i_][:, 0:nq], in_=PSB[b0 + i_][:, 0:nq], func=AF.Sigmoid,
                        bias=bg[:, 16 * i_ + fo:16 * i_ + fo + 1]), r=[f'PS{b0 + i_}', 'bg'], w=[f'T32_{i_}'])
                    op('dve', lambda e, i_=i_, nq=nq, b0=b0: e.tensor_tensor(
                        out=U32[i_][:, 0:nq], in0=T32[i_][:, 0:nq], in1=PSB[b0 + 2 + i_][:, 0:nq], op=ALU.mult),
                        r=[f'T32_{i_}', f'PS{b0 + 2 + i_}'], w=[f'U32_{i_}'])
                op('dve', lambda e, fo=fo, q0=q0, nq=nq: e.tensor_tensor(
                    out=mergedT[:, fo, q0:q0 + nq], in0=U32[0][:, 0:nq], in1=U32[1][:, 0:nq], op=ALU.add),
                    r=['U32_0', 'U32_1'], w=['mergedT'])
        def load_wout(g, ws):
            op('pool', lambda e: e.dma_start(
                out=WB[ws][:, :, :], in_=w_out_d[:, g * 512:(g + 1) * 512].rearrange("(k p) n -> p k n", p=128)),
                w=[f'WB{ws}a', f'WB{ws}b', f'WB{ws}c', f'WB{ws}d'], dma=f'WB{ws}a')

        load_wout(0, wc % 2)
        for g in range(4):
            ws = wc % 2
            wc += 1
            if g + 1 < 4:
                load_wout(g + 1, wc % 2)
            for tile in range(NTOWN):
                bank = tile % 2
                ts_ = slice(tile * 128, (tile + 1) * 128)
                for kc in range(KC):
                    op('pe', lambda e, ws=ws, kc=kc, bank=bank, ts_=ts_: e.matmul(
                        PSB[bank][:, :], lhsT=mergedT[:, kc, ts_], rhs=WB[ws][:, kc, :],
                        start=(kc == 0), stop=(kc == KC - 1)), r=[f'WB{ws}a', 'mergedT'], w=[f'PS{bank}'])
                op('sp', lambda e, bank=bank, ts_=ts_, g=g: e.dma_start(out=T32[bank][:, 0:512],
                                                                        in_=xown5_d[ts_, g * 512:(g + 1) * 512]),
                   w=[f'T32_{bank}'], dma=f'T32_{bank}')
                op('dve', lambda e, bank=bank, g=g: e.tensor_tensor(
                    out=U32[bank][:, 0:512], in0=PSB[bank][:, :], in1=G1B[:, g * 512:(g + 1) * 512], op=ALU.mult),
                    r=[f'PS{bank}', 'G1B'], w=[f'U32_{bank}'])
                op('dve', lambda e, bank=bank: e.tensor_tensor(
                    out=U32[bank][:, 512:1024], in0=U32[bank][:, 0:512], in1=T32[bank][:, 0:512], op=ALU.add),
                    r=[f'U32_{bank}', f'T32_{bank}'], w=[f'U32_{bank}'])
                op('pool', lambda e, bank=bank, ts_=ts_, g=g: e.dma_start(out=xnewD[ts_, g * 512:(g + 1) * 512],
                                                                        in_=U32[bank][:, 512:1024]),
                   r=[f'U32_{bank}'], w=[uniq()], dma=f'U32_{bank}')
        sch.barrier()

        phase(6)
        P6ON = start_at <= 6 <= stop_after
        h2T = av(0, [16, 1280])
        actT = av(20480, [44, 512])
        G2B = av(43008, [2048], F32)
        cw = av(47104, [88, 3], F32)
        cb = av(47632, [88], F32)
        if P6ON:
            w_up_d, w_down_d, cwT_d, cbT_d = I('w_up'), I('w_down'), I('cwT'), I('cbT')
        else:
            w_up_d = w_down_d = cwT_d = cbT_d = None
        bcast_row(modD[0:1, 5 * D:6 * D], G2B, 'G2B')
        op('sp', lambda e: e.dma_start(out=cw, in_=cwT_d[:, :, :]), w=['cw'], dma='cw')
        op('sp', lambda e: e.dma_start(out=cb, in_=cbT_d[:, :]), w=['cb'], dma='cb')
        k1(xnewD, NTOWN, None, lambda t: A2, lambda t: modT[:, 0, 3, :], 'h2T', dst_sb=h2T)
        for ci_, col in enumerate((127, 1152)):
            op('dve', lambda e, ci_=ci_, col=col: e.tensor_scalar(
                out=h2T[:, :, col:col + 1], in0=h2T[:, :, col:col + 1], scalar1=edge[:, ci_:ci_ + 1], scalar2=None,
                op0=ALU.mult), r=['h2T', 'edge'], w=['h2T'])
        if _os.environ.get('P6TAP'):
            op('sp', lambda e: e.dma_start(out=hTown.rearrange("k p t -> p k t"), in_=h2T), r=['h2T'], w=[uniq()], dma='h2Ttap')
        wc = 0
        for hf in range(2):
            c0 = 128 + hf * 512
            for j in range(NJ):
                ws = wc % 2
                wc += 1
                op('pool', lambda e, ws=ws, j=j: e.dma_start(
                    out=WB[ws][:, :, 0:128], in_=w_up_d[:, j * 128:(j + 1) * 128].rearrange("(k p) n -> p k n", p=128)),
                    w=[f'WB{ws}a'], dma=f'WB{ws}a')
                op('pool', lambda e, ws=ws, j=j: e.dma_start(
                    out=WB[ws][:, :, 128:256],
                    in_=w_up_d[:, DFF + j * 128:DFF + (j + 1) * 128].rearrange("(k p) n -> p k n", p=128)),
                    w=[f'WB{ws}b'], dma=f'WB{ws}b')
                b0 = 4 * (j % 2)
                for i_ in range(2):
                    wkey = f'WB{ws}' + 'ab'[i_]
                    for (bo, ca, cn) in ((0, c0 - 1, 512), (1, c0 + 511, 2)):
                        for kc in range(KC):
                            op('pe', lambda e, ws=ws, kc=kc, i_=i_, bo=bo, ca=ca, cn=cn, b0=b0: e.matmul(
                                PSB[b0 + 2 * i_ + bo][:, 0:cn], lhsT=WB[ws][:, kc, i_ * 128:(i_ + 1) * 128],
                                rhs=h2T[:, kc, ca:ca + cn], start=(kc == 0), stop=(kc == KC - 1)),
                                r=[wkey, 'h2T'], w=[f'PS{b0 + 2 * i_ + bo}'])
                    op('act', lambda e, i_=i_, b0=b0: e.activation(out=T32[i_][:, 0:512], in_=PSB[b0 + 2 * i_][:, 0:512],
                                                                   func=AF.Copy), r=[f'PS{b0 + 2 * i_}'], w=[f'T32_{i_}'])
                    op('act', lambda e, i_=i_, b0=b0: e.activation(out=T32[i_][:, 512:514], in_=PSB[b0 + 2 * i_ + 1][:, 0:2],
                                                                   func=AF.Copy), r=[f'PS{b0 + 2 * i_ + 1}'], w=[f'T32_{i_}'])
                    jj = i_ * NJ + j
                    op('dve', lambda e, i_=i_, jj=jj: e.tensor_scalar(
                        out=U32[i_][:, 0:512], in0=T32[i_][:, 1:513], scalar1=cw[:, jj, 1:2], scalar2=None, op0=ALU.mult),
                        r=[f'T32_{i_}', 'cw'], w=[f'U32_{i_}'])
                    for tap, off in ((0, 0), (2, 2)):
                        op('dve', lambda e, i_=i_, jj=jj, tap=tap, off=off: e.scalar_tensor_tensor(
                            out=U32[i_][:, 0:512], in0=T32[i_][:, off:off + 512], scalar=cw[:, jj, tap:tap + 1],
                            in1=U32[i_][:, 0:512], op0=ALU.mult, op1=ALU.add),
                            r=[f'T32_{i_}', 'cw', f'U32_{i_}'], w=[f'U32_{i_}'])
                op('act', lambda e, j=j: e.activation(out=U32[0][:, 512:1024], in_=U32[0][:, 0:512], func=AF.Silu,
                                                      bias=cb[:, j:j + 1]), r=['U32_0', 'cb'], w=['sg'])
                op('dve', lambda e, j=j: e.scalar_tensor_tensor(
                    out=actT[:, j, :], in0=U32[1][:, 0:512], scalar=cb[:, NJ + j:NJ + j + 1], in1=U32[0][:, 512:1024],
                    op0=ALU.add, op1=ALU.mult), r=['U32_1', 'sg', 'cb'], w=['actT'])
            if _os.environ.get('P6TAP') and hf == 1:
                dbgA = dscr('dbgA', [NJ, 128, 512], BF16)
                op('sp', lambda e: e.dma_start(out=dbgA.rearrange("j p t -> p j t"), in_=actT), r=['actT'], w=[uniq()], dma='h2Ttap')
            for cg4 in range(4):
                bs = 4 * (cg4 % 2)
                for cgi in range(4):
                    cg = cg4 * 4 + cgi
                    xs = wc % 2
                    wc += 1
                    WD = XB[xs][:].rearrange("p a b -> p (a b)")[:, 0:NJ * 128].rearrange("p (j n) -> p j n", n=128)
                    op('pool', lambda e, WD=WD, cg=cg: e.dma_start(
                        out=WD, in_=w_down_d[:, cg * 128:(cg + 1) * 128].rearrange("(j p) n -> p j n", p=128)),
                        w=[f'XB{xs}'], dma=f'XB{xs}')
                    for ti in range(4):
                        for j in range(NJ):
                            op('pe', lambda e, WD=WD, ti=ti, j=j, bs=bs, cgi=cgi: e.matmul(
                                PSB[bs + ti][:, cgi * 128:(cgi + 1) * 128], lhsT=actT[:, j, ti * 128:(ti + 1) * 128],
                                rhs=WD[:, j, :], start=(j == 0), stop=(j == NJ - 1)),
                                r=[f'XB{xs}', 'actT'], w=[f'PS{bs + ti}'])
                for ti in range(4):
                    et = 1 + hf * 4 + ti
                    yr = slice((hf * 4 + ti) * 128, (hf * 4 + ti + 1) * 128)
                    gs = slice(cg4 * 512, (cg4 + 1) * 512)
                    sl = ti % 2
                    op('sp', lambda e, et=et, gs=gs, sl=sl: e.dma_start(out=T32[sl][:, 1024:1536],
                                                                        in_=xnewD[et * 128:(et + 1) * 128, gs]),
                       w=[f'T32_{sl}'], dma=f'T32_{sl}x')
                    op('dve', lambda e, ti=ti, bs=bs, gs=gs, sl=sl: e.tensor_tensor(
                        out=U32[sl][:, 1024:1536], in0=PSB[bs + ti][:, :], in1=G2B[:, gs], op=ALU.mult),
                        r=[f'PS{bs + ti}', 'G2B'], w=[f'U32_{sl}'])
                    op('dve', lambda e, sl=sl: e.tensor_tensor(
                        out=U32[sl][:, 1536:2048], in0=U32[sl][:, 1024:1536], in1=T32[sl][:, 1024:1536], op=ALU.add),
                        r=[f'U32_{sl}', f'T32_{sl}'], w=[f'U32_{sl}'])
                    op('act', lambda e, yr=yr, gs=gs, sl=sl: e.dma_start(out=y[yr, gs], in_=U32[sl][:, 1536:2048]),
                       r=[f'U32_{sl}'], w=[uniq()], dma=f'U32_{sl}z')
        sch.barrier()

        phase(None)
        if stop_after <= 5:
            op('dve', lambda e: e.memset(T32[0][:], 0.0), w=['T32_0'])
            op('sp', lambda e: e.dma_start(out=y[0:128, :], in_=T32[0][:]), r=['T32_0'], w=[uniq()], dma='T32_0')
        sch.barrier()
        sch.emit()
    return nc, list(declared)


def host_inputs(inputs, core):
    b, tq = core // 4, core % 4
    f = lambda a: np.ascontiguousarray(np.asarray(a, dtype=np.float32))
    x = f(inputs["x"])[b]
    ctx = f(inputs["ctx"])[b]
    pad = np.zeros((128, D), np.float32)
    xpad = np.concatenate([pad, x, pad], 0)
    m = {}
    m["xall"] = np.concatenate([ctx, xpad], 0)
    m["xown"] = xpad[tq * 1024: tq * 1024 + TOWN]
    c2 = np.stack([f(inputs["c"])[b], f(inputs["c_ctx"])], -1)
    m["cT"] = np.ascontiguousarray(c2.reshape(KC, 128, 2).transpose(1, 0, 2))
    m["w_ada"] = f(inputs["w_ada"])[0]
    m["b_ada2"] = np.ascontiguousarray(np.broadcast_to(f(inputs["b_ada"])[0][None], (2, 6 * D)))
    m["nw1T"] = np.ascontiguousarray(f(inputs["norm1_w"])[0].reshape(KC, 128).T)
    m["nw2T"] = np.ascontiguousarray(f(inputs["norm2_w"])[0].reshape(KC, 128).T)
    m["ident"] = np.eye(128, dtype=np.float32)
    m["w_in"] = f(inputs["w_in"])[0]
    bc = lambda v, rep: np.ascontiguousarray(np.broadcast_to(np.tile(f(v), rep)[None], (128, f(v).size * rep)))
    m["qnw"] = bc(inputs["diff_qnorm_w"][0], 8)
    m["knw"] = bc(inputs["diff_knorm_w"][0], 8)
    inv = (10000.0 ** (-np.arange(16, dtype=np.float32) / 16)).astype(np.float32)

    def rope_tab(tok):
        valid = (tok >= 0) & (tok < S)
        tt = np.where(valid, tok, 0)
        ar = ((tt // 64).astype(np.float32)[:, None] * inv).astype(np.float32)
        ac = ((tt % 64).astype(np.float32)[:, None] * inv).astype(np.float32)
        cr, sr, cc, sc = np.cos(ar), np.sin(ar), np.cos(ac), np.sin(ac)
        C = np.concatenate([cr, cr, cc, cc], 1)
        Sg = np.concatenate([-sr, sr, -sc, sc], 1)
        C = np.where(valid[:, None], C, 1.0).astype(np.float32)
        Sg = np.where(valid[:, None], Sg, 0.0).astype(np.float32)
        return np.ascontiguousarray(np.tile(C, (1, 8))), np.ascontiguousarray(np.tile(Sg, (1, 8)))
    def aug(wu, bu):
        a = np.zeros((33, 512), np.float32)
        a[0:16] = f(wu)[0]
        a[32] = f(bu)[0]
        return a
    m["waf"] = aug(inputs["w_a_up_f"], inputs["b_a_f"])
    m["wab"] = aug(inputs["w_a_up_b"], inputs["b_a_b"])
    jj, ii = np.meshgrid(np.arange(128), np.arange(128), indexing="ij")
    same = (jj // 64) == (ii // 64)
    m["mskf"] = (same & (jj <= ii)).astype(np.float32)
    m["mskb"] = (same & (jj >= ii)).astype(np.float32)
    m["gonw"] = bc(inputs["gla_onorm_w"][0], 8)
    e0 = 2 + 8 * tq
    real = lambda g_: 3 <= g_ <= 34
    mc = np.zeros((3 * NTALL,), np.float32)
    for t_ in range(NTALL):
        mc[t_] = 1.0 if (t_ < 2 or (real(t_) and t_ < e0)) else 0.0
        mc[NTALL + t_] = 1.0 if (t_ < 2 or (real(t_) and t_ > e0 + 9)) else 0.0
    for e_ in range(NTOWN):
        mc[2 * NTALL + e_] = 1.0 if real(e0 + e_) else 0.0
    m["mcols"] = np.ascontiguousarray(np.broadcast_to(mc[None], (128, 3 * NTALL)))
    m["edge"] = np.ascontiguousarray(np.broadcast_to(
        np.array([0.0 if tq == 0 else 1.0, 0.0 if tq == 3 else 1.0], np.float32)[None], (128, 2)))
    m["donw"] = bc(inputs["diff_onorm_w"][0], 1)
    lv = np.stack([f(inputs[k_])[0] for k_ in ("lambda_q1", "lambda_k1", "lambda_q2", "lambda_k2")], 0)
    m["lamv"] = np.ascontiguousarray(np.broadcast_to(lv[None], (128, 4, 64)))
    m["w_pg"] = f(inputs["w_proj_gla"])[0]
    m["w_pd"] = f(inputs["w_proj_diff"])[0]
    m["w_gate"] = f(inputs["w_gate"])[0]
    m["bgT"] = np.ascontiguousarray(f(inputs["b_gate"])[0].reshape(32, 128).T)
    m["w_out"] = f(inputs["w_out"])[0]
    m["w_up"] = f(inputs["w_up"])[0]
    m["w_down"] = f(inputs["w_down"])[0]
    m["cwT"] = np.ascontiguousarray(f(inputs["conv_w"])[0].reshape(3, 88, 128).transpose(2, 1, 0))
    m["cbT"] = np.ascontiguousarray(f(inputs["conv_b"])[0].reshape(88, 128).T)
    pall = np.arange(TALL) - 384
    pall[:256] = -1
    m["ropeC"], m["ropeS"] = rope_tab(pall)
    m["ropeCo"], m["ropeSo"] = rope_tab(np.arange(TOWN) + tq * 1024 - 128)
    return m


_NC_CACHE = {}


def kernel(**inputs):
    if "nc" not in _NC_CACHE:
        _NC_CACHE["nc"] = build()
    nc, decl = _NC_CACHE["nc"]
    maps = []
    for core in range(8):
        m = host_inputs(inputs, core)
        maps.append({k: m[k] for k in decl})
    res = run_bass_kernel_spmd(nc, maps, core_ids=list(range(8)))
    out = np.zeros((2, S, D), np.float32)
    for core in range(8):
        b, tq = core // 4, core % 4
        out[b, tq * 1024:(tq + 1) * 1024, :] = np.asarray(res.results[core]["y"], dtype=np.float32)
    return out
```

```python
import numpy as np
from contextlib import ExitStack
import concourse.bass as bass
import concourse.mybir as mybir
from concourse.bass_utils import run_bass_kernel_spmd

F32 = mybir.dt.float32
BF16 = mybir.dt.bfloat16
AF = mybir.ActivationFunctionType
ALU = mybir.AluOpType
AX = mybir.AxisListType

D = 2048
KC = 16
S = 4096
CTX = 256
TALL = 4608
NTALL = 36
TOWN = 1280
NTOWN = 10
DFF = 5632
NJ = 44
EPS = 1e-6
LAM_INIT = 0.8 - 0.6 * 1.0
ENGS = ['pe', 'act', 'dve', 'pool', 'sp']


class Sched:
    def __init__(self, nc, es):
        self.nc = nc
        self.ops = {e: [] for e in ENGS}
        self.res = {}
        self.sems = [es.enter_context(nc.semaphore(f"sm{i}")) for i in range(100)]
        self.nsem = 0
        self.prog = {}
        for e in ENGS:
            self.prog[e] = self.sems[self.nsem]
            self.nsem += 1
        self.slot = {}
        self.last = {}
        self.active = True

    def _slot(self, key):
        if key not in self.slot:
            self.slot[key] = [self.nsem, 0]
            self.nsem += 1
            assert self.nsem <= len(self.sems), "out of semaphores"
        return self.slot[key]

    def op(self, eng, fn, r=(), w=(), dma=None):
        if not self.active:
            return
        idx = len(self.ops[eng])
        if dma is not None:
            sl = self._slot(dma)
            sl[1] += 1
            ev = ('d', sl[0], 16 * sl[1])
            src = ('d', sl[0])
        else:
            ev = ('c', eng, idx)
            src = eng
        waits = []
        for k in r:
            st = self.res.setdefault(k, {'w': {}, 'r': {}})
            for s2, e2 in st['w'].items():
                if s2 == eng and eng == 'pe' and dma is None:
                    continue
                waits.append(e2)
        for k in w:
            st = self.res.setdefault(k, {'w': {}, 'r': {}})
            for dd in (st['w'], st['r']):
                for s2, e2 in dd.items():
                    if s2 == eng and dma is None:
                        continue
                    waits.append(e2)
        for k in r:
            self.res[k]['r'][src] = ev
        for k in w:
            self.res[k]['w'] = {src: ev}
            self.res[k]['r'] = {}
        self.last[src] = ev
        self.ops[eng].append({'fn': fn, 'waits': waits, 'dma': (self.sems[ev[1]] if dma is not None else None),
                              'ms': False})

    def barrier(self):
        if not self.active:
            return
        evs = list(self.last.values())
        for e in ENGS:
            self.ops[e].append({'fn': None, 'waits': list(evs), 'dma': None, 'ms': False})
        self.res = {}

    def emit(self):
        nc = self.nc
        for e in ENGS:
            for o in self.ops[e]:
                for ev in o['waits']:
                    if ev[0] == 'c':
                        self.ops[ev[1]][ev[2]]['ms'] = True
        rank = {}
        for e in ENGS:
            c = 0
            for i, o in enumerate(self.ops[e]):
                if o['ms']:
                    c += 1
                    rank[(e, i)] = c
        prog = self.prog
        sems = self.sems

        def run(ename, eng):
            known = {}
            for o in self.ops[ename]:
                need = {}
                for ev in o['waits']:
                    if ev[0] == 'c':
                        sid, val = id(prog[ev[1]]), rank[(ev[1], ev[2])]
                        sem = prog[ev[1]]
                    else:
                        sem = sems[ev[1]]
                        sid, val = id(sem), ev[2]
                    if sid not in need or need[sid][1] < val:
                        need[sid] = (sem, val)
                for sid, (sem, val) in need.items():
                    if known.get(sid, 0) < val:
                        eng.wait_ge(sem, val)
                        known[sid] = val
                if o['fn'] is not None:
                    ins = o['fn'](eng)
                    if o['dma'] is not None:
                        ins.then_inc(o['dma'], 16)
                    elif o['ms']:
                        ins.then_inc(prog[ename], 1)

        with nc.Block() as block:
            block.tensor(lambda e: run('pe', e))
            block.scalar(lambda e: run('act', e))
            block.vector(lambda e: run('dve', e))
            block.gpsimd(lambda e: run('pool', e))
            block.sync(lambda e: run('sp', e))


def build(stop_after=99, taps=(), start_at=0, feeds=()):
    nc = bass.Bass("TRN2", target_bir_lowering=False)
    es = ExitStack()
    with es:
        def din(name, shape, dt=F32):
            return nc.dram_tensor(name, list(shape), dt, kind="ExternalInput").ap()

        def dscr(name, shape, dt):
            kind = "ExternalOutput" if name in taps else ("ExternalInput" if name in feeds else "Internal")
            return nc.dram_tensor(name, list(shape), dt, kind=kind).ap()

        ISPEC = {
            'xall': [TALL, D],
            'xown': [TOWN, D],
            'cT': [128, KC, 2],
            'w_ada': [D, 6 * D],
            'b_ada2': [2, 6 * D],
            'nw1T': [128, KC],
            'nw2T': [128, KC],
            'w_in': [D, 6176],
            'waf': [33, 512],
            'wab': [33, 512],
            'ident': [128, 128],
            'trif': [128, 128],
            'trib': [128, 128],
            'mskf': [128, 128],
            'mskb': [128, 128],
            'mcols': [128, 3 * NTALL],
            'edge': [128, 2],
            'ropeC': [TALL, 512],
            'ropeS': [TALL, 512],
            'ropeCo': [TOWN, 512],
            'ropeSo': [TOWN, 512],
            'qnw': [128, 512],
            'knw': [128, 512],
            'gonw': [128, 1024],
            'donw': [128, 128],
            'lamv': [128, 4, 64],
            'w_pg': [1024, D],
            'w_pd': [1024, D],
            'w_gate': [D, 2 * D],
            'bgT': [128, 32],
            'w_out': [D, D],
            'w_up': [D, 2 * DFF],
            'cwT': [128, 88, 3],
            'cbT': [128, 88],
            'w_down': [DFF, D],
        }
        declared = {}

        def I(name):
            if name not in declared:
                declared[name] = din(name, ISPEC[name])
            return declared[name]
        PH_INPUTS = {6: ['w_up','w_down','cwT','cbT'], 5: ['w_gate','w_pg','w_pd','w_out','bgT','xown'], 4: ['donw','lamv'], 3: ['waf','wab','mskf','mskb','gonw'], 2: ['w_in','qnw','knw','ropeC','ropeS','ropeCo','ropeSo'], 1: ['cT','w_ada','b_ada2','xall','xown'], -1: ['ident','nw1T','nw2T','mcols','edge']}
        for ph, names in PH_INPUTS.items():
            if ph == -1 or start_at <= ph <= stop_after:
                for n_ in names:
                    I(n_)

        y = nc.dram_tensor("y", [1024, D], F32, kind="ExternalOutput").ap()

        modD = dscr("modD", [2, 6 * D], F32)
        hTall = dscr("hTall", [KC, 128, TALL], BF16)
        hTown = dscr("hTown", [KC, 128, TOWN], BF16)
        gkTa = dscr("gkTa", [4, 128, TALL], BF16)
        gva = dscr("gva", [TALL, 1024], BF16)
        gafTa = dscr("gafTa", [16, TALL], BF16)
        gabTa = dscr("gabTa", [16, TALL], BF16)
        dkT = dscr("dkT", [8, 128, TALL], BF16)
        dv1 = dscr("dv1", [TALL, 8, 129], BF16)
        gqTo = dscr("gqTo", [4, 128, TOWN], BF16)
        gkTo = dscr("gkTo", [4, 128, TOWN], BF16)
        gvo = dscr("gvo", [TOWN, 1024], BF16)
        gro = dscr("gro", [TOWN, 1024], BF16)
        gafTo = dscr("gafTo", [16, TOWN], BF16)
        gabTo = dscr("gabTo", [16, TOWN], BF16)
        dqT = dscr("dqT", [8, 128, TOWN], BF16)
        yaT = dscr("yaT", [8, 128, TOWN], BF16)
        ybT = dscr("ybT", [8, 128, TOWN], BF16)
        xnewD = dscr("xnewD", [TOWN, D], F32)

        sb = lambda name, shape, dt: es.enter_context(nc.sbuf_tensor(name, list(shape), dt))
        ps = lambda name, shape, dt: es.enter_context(nc.psum_tensor(name, list(shape), dt))
        sch = Sched(nc, es)
        op = sch.op
        _u = [0]

        def uniq():
            _u[0] += 1
            return f'dram{_u[0]}'

        ident = sb("ident_s", [128, 128], BF16)
        modT = sb("modT", [128, 2, 6, KC], F32)
        A1 = sb("A1", [128, 2, KC], F32)
        A2 = sb("A2", [128, KC], F32)
        nw1s = sb("nw1s", [128, KC], F32)
        nw2s = sb("nw2s", [128, KC], F32)
        mcols = sb("mcols_s", [128, 3 * NTALL], F32)
        edge = sb("edge_s", [128, 2], F32)
        stat = sb("stat", [128, 64], F32)

        XB = [sb(f"XB{i}", [128, KC, 512], BF16) for i in range(2)]
        WB = [sb(f"WB{i}", [128, KC, 512], BF16) for i in range(2)]
        T32 = [sb(f"T32_{i}", [128, 2048], F32) for i in range(2)]
        U32 = [sb(f"U32_{i}", [128, 2048], F32) for i in range(2)]
        T16 = [sb(f"T16_{i}", [128, 2048], BF16) for i in range(2)]
        SG = [sb(f"SG{i}", [128, 1040], BF16) for i in range(2)]
        PSB = [ps(f"PS{i}", [128, 512], F32) for i in range(8)]

        ARENA = sb("ARENA", [128, 48128], BF16)

        def av(off, shape, dt=BF16):
            n = int(np.prod(shape)) * (2 if dt is F32 else 1)
            ap = ARENA[:, off:off + n]
            if dt is F32:
                ap = ap.bitcast(F32)
            if len(shape) == 2:
                ap = ap.rearrange("p (a b) -> p a b", b=shape[1])
            elif len(shape) == 3:
                ap = ap.rearrange("p (a b c) -> p a b c", b=shape[1], c=shape[2])
            return ap

        def psbf(i):
            return PSB[i][:].bitcast(BF16)

        def phase(k):
            sch.active = (k is None) or (start_at <= k <= stop_after)

        phase(None)
        op('pool', lambda e: e.dma_start(out=ident[:], in_=I('ident')[:, :]), w=['ident'], dma='ident')
        op('sp', lambda e: e.dma_start(out=nw1s[:], in_=I('nw1T')[:, :]), w=['nw1s'], dma='nw1s')
        op('sp', lambda e: e.dma_start(out=nw2s[:], in_=I('nw2T')[:, :]), w=['nw2s'], dma='nw2s')
        op('sp', lambda e: e.dma_start(out=mcols[:], in_=I('mcols')[:, :]), w=['mcols'], dma='mcols')
        op('sp', lambda e: e.dma_start(out=edge[:], in_=I('edge')[:, :]), w=['edge'], dma='edge')
        epsc = sb("epsc", [128, 1], F32)
        op('dve', lambda e: e.memset(epsc[:], EPS), w=['epsc'])
        phase(1)
        cTs = T32[0][:, 0:32].rearrange("p (k b) -> p k b", b=2)
        scT = T16[0][:, 0:32].rearrange("p (k b) -> p k b", b=2)
        op('sp', lambda e: e.dma_start(out=cTs, in_=I('cT')[:, :, :]), w=['T32_0'], dma='T32_0')
        op('act', lambda e: e.activation(out=scT, in_=cTs, func=AF.Silu), r=['T32_0'], w=['T16_0'])
        bada = U32[0]
        for g in range(24):
            wslot = g % 2
            wk = f'WB{wslot}'
            op('pool', lambda e, g=g, wslot=wslot: e.dma_start(
                out=WB[wslot][:], in_=I('w_ada')[:, g * 512:(g + 1) * 512].rearrange("(k p) n -> p k n", p=128)),
                w=[wk], dma=wk)
            bk = f'U32_{g % 2}'
            op('sp', lambda e, g=g: e.dma_start(out=U32[g % 2][0:2, 0:512], in_=I('b_ada2')[:, g * 512:(g + 1) * 512]),
               w=[bk], dma=bk)
            pk = f'PS{g % 2}'
            for kc in range(KC):
                op('pe', lambda e, g=g, kc=kc, wslot=wslot: e.matmul(
                    PSB[g % 2][0:2, :], lhsT=scT[:, kc, :], rhs=WB[wslot][:, kc, :],
                    start=(kc == 0), stop=(kc == KC - 1)),
                    r=[wk, 'T16_0'], w=[pk])
            sk = f'T32_1s{g % 2}'
            op('dve', lambda e, g=g: e.tensor_tensor(
                out=T32[1][0:2, (g % 2) * 512:(g % 2) * 512 + 512], in0=PSB[g % 2][0:2, :],
                in1=U32[g % 2][0:2, 0:512], op=ALU.add), r=[pk, bk], w=[sk])
            op('sp', lambda e, g=g: e.dma_start(
                out=modD[:, g * 512:(g + 1) * 512], in_=T32[1][0:2, (g % 2) * 512:(g % 2) * 512 + 512]),
                r=[sk], w=[uniq()], dma=sk)
        sch.barrier()
        phase(None)
        for v in range(2):
            for s_ in range(6):
                op('sp', lambda e, v=v, s_=s_: e.dma_start(
                    out=modT[:, v, s_, :],
                    in_=modD[v:v + 1, s_ * D:(s_ + 1) * D].rearrange("o (k p) -> p (o k)", p=128),
                    allow_slow_non_contiguous=True), w=['modT'], dma=f'modT{v}{s_}')
        for v in range(2):
            op('dve', lambda e, v=v: e.scalar_tensor_tensor(
                out=A1[:, v, :], in0=modT[:, v, 1, :], scalar=1.0, in1=nw1s[:], op0=ALU.add, op1=ALU.mult),
                r=['modT', 'nw1s'], w=['A1'])
        op('dve', lambda e: e.scalar_tensor_tensor(
            out=A2[:], in0=modT[:, 0, 4, :], scalar=1.0, in1=nw2s[:], op0=ALU.add, op1=ALU.mult),
            r=['modT', 'nw2s'], w=['A2'])
        sch.barrier()

        def k1(src, ntiles, dst, A_of, B_of, tagp, dst_sb=None):
            nblk = (ntiles + 3) // 4
            for b in range(nblk):
                t0 = b * 4
                nt = min(4, ntiles - t0)
                xb = b % 2
                xbk = f'XB{xb}'
                for ti in range(nt):
                    t = t0 + ti
                    sl = t % 2
                    op('sp', lambda e, t=t, sl=sl: e.dma_start(out=T32[sl][:], in_=src[t * 128:(t + 1) * 128, :]),
                       w=[f'T32_{sl}'], dma=f'T32_{sl}')
                    op('act', lambda e, t=t, sl=sl: e.activation(
                        out=U32[sl][:], in_=T32[sl][:], func=AF.Square, accum_out=stat[:, sl:sl + 1]),
                        r=[f'T32_{sl}'], w=[f'U32_{sl}', f'st{sl}'])
                    op('act', lambda e, sl=sl: e.activation(
                        out=stat[:, 2 + sl:3 + sl], in_=stat[:, sl:sl + 1], func=AF.Sqrt, scale=1.0 / D, bias=epsc[:]),
                        r=[f'st{sl}', 'epsc'], w=[f'sd{sl}'])
                    op('dve', lambda e, sl=sl: e.reciprocal(out=stat[:, 4 + sl:5 + sl], in_=stat[:, 2 + sl:3 + sl]),
                       r=[f'sd{sl}'], w=[f'rs{sl}'])
                    op('act', lambda e, sl=sl: e.activation(
                        out=T16[sl][:], in_=T32[sl][:], func=AF.Copy, scale=stat[:, 4 + sl:5 + sl]),
                        r=[f'T32_{sl}', f'rs{sl}'], w=[f'T16_{sl}'])
                    for half in range(2):
                        bank = 2 * sl + half
                        for k8 in range(8):
                            kc = half * 8 + k8
                            op('pe', lambda e, sl=sl, kc=kc, bank=bank, k8=k8: e.transpose(
                                out=psbf(bank)[:, k8 * 128:(k8 + 1) * 128], in_=T16[sl][:, kc * 128:(kc + 1) * 128],
                                identity=ident[:]),
                                r=[f'T16_{sl}', 'ident'], w=[f'PS{bank}'])
                        for k8 in range(8):
                            kc = half * 8 + k8
                            op('dve', lambda e, t=t, ti=ti, xb=xb, kc=kc, bank=bank, k8=k8: e.tensor_scalar(
                                out=(XB[xb][:, kc, ti * 128:(ti + 1) * 128] if dst_sb is None
                                     else dst_sb[:, kc, t * 128:(t + 1) * 128]),
                                in0=psbf(bank)[:, k8 * 128:(k8 + 1) * 128],
                                scalar1=A_of(t)[:, kc:kc + 1], scalar2=B_of(t)[:, kc:kc + 1],
                                op0=ALU.mult, op1=ALU.add),
                                r=[f'PS{bank}', 'A1', 'A2', 'modT'], w=[xbk if dst_sb is None else tagp])
                if dst_sb is not None:
                    continue
                op('pool', lambda e, xb=xb, t0=t0, nt=nt: e.dma_start(
                    out=dst[:, :, t0 * 128:(t0 + nt) * 128].rearrange("k p t -> p k t"),
                    in_=XB[xb][:, :, 0:nt * 128]), r=[xbk], w=[uniq()], dma=xbk)

        phase(1)
        if start_at <= 1 <= stop_after:
            xall_d, xown_d = I('xall'), I('xown')
        else:
            xall_d = xown_d = None
        k1(xall_d, NTALL, hTall, lambda t: A1[:, (1 if t < 2 else 0), :], lambda t: modT[:, (1 if t < 2 else 0), 0, :],
           'hTall')
        k1(xown_d, NTOWN, hTown, lambda t: A1[:, 0, :], lambda t: modT[:, 0, 0, :], 'hTown')
        sch.barrier()


        phase(None)
        P2ON = start_at <= 2 <= stop_after
        cnt = {'w': 0, 'x': 0, 'b': 0, 's': 0}
        wqn = av(4096, [512], F32)
        wkn = av(5120, [512], F32)
        if P2ON:
            qnw_d, knw_d, w_in_d = I('qnw'), I('knw'), I('w_in')
            op('sp', lambda e: e.dma_start(out=wqn[:], in_=qnw_d[:, :]), w=['wqn'], dma='wqn')
            op('sp', lambda e: e.dma_start(out=wkn[:], in_=knw_d[:, :]), w=['wkn'], dma='wkn')
        RC = [av(i * 1024, [512], F32) for i in range(2)]
        RS = [av(2048 + i * 1024, [512], F32) for i in range(2)]
        onec = sb("onec", [128, 1], F32)
        op('dve', lambda e: e.memset(onec[:], 1.0), w=['onec'])
        op('dve', lambda e: e.memset(stat[:, 32:36], 0.0), w=['onec'])

        def gemm(XT, T, col0, ncols, mode, epi, M=128):
            ws = cnt['w'] % 2
            cnt['w'] += 1
            wk = f'WB{ws}'
            op('pool', lambda e: e.dma_start(
                out=WB[ws][:, :, 0:ncols],
                in_=w_in_d[:, col0:col0 + ncols].rearrange("(k p) n -> p k n", p=128)), w=[wk], dma=wk)
            nblk = (T + 511) // 512
            for blk in range(nblk):
                ntok = min(512, T - blk * 512)
                xs = cnt['x'] % 2
                cnt['x'] += 1
                xk = f'XB{xs}'
                op('sp', lambda e, blk=blk, ntok=ntok, xs=xs: e.dma_start(
                    out=XB[xs][:, :, 0:ntok],
                    in_=XT[:, :, blk * 512:blk * 512 + ntok].rearrange("k p t -> p k t")), w=[xk], dma=xk)
                if mode == 'tm':
                    for ti in range(ntok // 128):
                        bank = cnt['b'] % 4
                        cnt['b'] += 1
                        for kc in range(KC):
                            op('pe', lambda e, kc=kc, ti=ti, bank=bank, xs=xs: e.matmul(
                                PSB[bank][:, 0:ncols], lhsT=XB[xs][:, kc, ti * 128:(ti + 1) * 128],
                                rhs=WB[ws][:, kc, 0:ncols], start=(kc == 0), stop=(kc == KC - 1)),
                                r=[xk, wk], w=[f'PS{bank}'])
                        epi(blk * 4 + ti, bank)
                else:
                    for mc in range((ncols + M - 1) // M):
                        bank = cnt['b'] % 4
                        cnt['b'] += 1
                        for kc in range(KC):
                            op('pe', lambda e, kc=kc, mc=mc, bank=bank, xs=xs, ntok=ntok: e.matmul(
                                PSB[bank][0:M, 0:ntok], lhsT=WB[ws][:, kc, mc * M:(mc + 1) * M],
                                rhs=XB[xs][:, kc, 0:ntok], start=(kc == 0), stop=(kc == KC - 1)),
                                r=[xk, wk], w=[f'PS{bank}'])
                        epi(mc, blk, ntok, bank)
            if hasattr(epi, 'flush'):
                epi.flush()

        def epi_fm(dst_of, M=128):
            def f(mc, blk, ntok, bank):
                s_ = cnt['s'] % 2
                cnt['s'] += 1
                op('act', lambda e: e.activation(out=SG[s_][0:M, 0:ntok], in_=PSB[bank][0:M, 0:ntok], func=AF.Copy),
                   r=[f'PS{bank}'], w=[f'SG{s_}'])
                op('act', lambda e: e.dma_start(out=dst_of(mc)[:, blk * 512:blk * 512 + ntok], in_=SG[s_][0:M, 0:ntok]),
                   r=[f'SG{s_}'], w=[uniq()], dma=f'SG{s_}')
            return f

        def epi_tm(dst, cofs):
            def f(tile, bank):
                s_ = cnt['s'] % 2
                cnt['s'] += 1
                op('act', lambda e: e.activation(out=SG[s_][:, 0:512], in_=PSB[bank][:, :], func=AF.Copy),
                   r=[f'PS{bank}'], w=[f'SG{s_}'])
                op('act', lambda e: e.dma_start(out=dst[tile * 128:(tile + 1) * 128, cofs:cofs + 512], in_=SG[s_][:, 0:512]),
                   r=[f'SG{s_}'], w=[uniq()], dma=f'SG{s_}')
            return f

        def epi_dv(g):
            def f(tile, bank):
                s_ = cnt['s'] % 2
                cnt['s'] += 1
                sg3 = SG[s_][:, 0:516].rearrange("p (h d) -> p h d", d=129)
                op('act', lambda e: e.activation(out=sg3[:, :, 0:128],
                                                 in_=PSB[bank][:, :].rearrange("p (h d) -> p h d", d=128), func=AF.Copy),
                   r=[f'PS{bank}'], w=[f'SG{s_}'])
                op('act', lambda e: e.activation(out=sg3[:, :, 128], in_=stat[:, 32:36], func=AF.Identity,
                                                 scale=0.0, bias=onec[:]),
                   r=['onec'], w=[f'SG{s_}'])
                op('act', lambda e: e.dma_start(out=dv1[tile * 128:(tile + 1) * 128, g * 4:(g + 1) * 4, :], in_=sg3),
                   r=[f'SG{s_}'], w=[uniq()], dma=f'SG{s_}')
            return f

        def epi_qk(g, wn, wnk, ropeCd, ropeSd, dst):
            pend = []

            def stage_a(tile, bank, s_):
                psb = PSB[bank]
                c0_ = 8 if s_ == 0 else 40
                op('sp', lambda e: e.dma_start(out=RC[s_][:], in_=ropeCd[tile * 128:(tile + 1) * 128, :]),
                   w=[f'RC{s_}'], dma=f'RC{s_}')
                op('sp', lambda e: e.dma_start(out=RS[s_][:], in_=ropeSd[tile * 128:(tile + 1) * 128, :]),
                   w=[f'RS{s_}'], dma=f'RS{s_}')
                op('act', lambda e: e.activation(out=U32[s_][:, 0:512], in_=psb[:, :], func=AF.Square),
                   r=[f'PS{bank}'], w=[f'U32_{s_}'])
                op('dve', lambda e: e.tensor_reduce(
                    out=stat[:, c0_:c0_ + 8], in_=U32[s_][:, 0:512].rearrange("p (g d) -> p g d", d=64), axis=AX.X,
                    op=ALU.add), r=[f'U32_{s_}'], w=[f'qst{s_}'])
                op('act', lambda e: e.activation(out=stat[:, c0_ + 8:c0_ + 16], in_=stat[:, c0_:c0_ + 8], func=AF.Sqrt,
                                                 scale=1.0 / 64, bias=epsc[:]), r=[f'qst{s_}', 'epsc'], w=[f'qsd{s_}'])
                op('dve', lambda e: e.reciprocal(out=stat[:, c0_ + 16:c0_ + 24], in_=stat[:, c0_ + 8:c0_ + 16]),
                   r=[f'qsd{s_}'], w=[f'qrs{s_}'])

            def stage_b(tile, bank, s_):
                psb = PSB[bank]
                c0_ = 8 if s_ == 0 else 40
                for gg in range(8):
                    op('dve', lambda e, gg=gg: e.scalar_tensor_tensor(
                        out=T32[s_][:, gg * 64:(gg + 1) * 64], in0=psb[:, gg * 64:(gg + 1) * 64],
                        scalar=stat[:, c0_ + 16 + gg:c0_ + 17 + gg], in1=wn[:, gg * 64:(gg + 1) * 64],
                        op0=ALU.mult, op1=ALU.mult), r=[f'PS{bank}', f'qrs{s_}', wnk], w=[f'T32_{s_}'])
                kn = T32[s_][:, 0:512]
                t1 = U32[s_][:, 512:1024]
                t2 = U32[s_][:, 1024:1536]
                op('dve', lambda e: e.tensor_tensor(out=t1, in0=kn, in1=RC[s_][:], op=ALU.mult),
                   r=[f'T32_{s_}', f'RC{s_}'], w=[f'U32b_{s_}'])
                v5 = lambda ap: ap.rearrange("p (g h d) -> p g h d", h=2, d=16)
                for hh in range(2):
                    op('dve', lambda e, hh=hh: e.tensor_tensor(
                        out=v5(t2)[:, :, hh, :], in0=v5(kn)[:, :, 1 - hh, :], in1=v5(RS[s_][:])[:, :, hh, :], op=ALU.mult),
                        r=[f'T32_{s_}', f'RS{s_}'], w=[f'U32b_{s_}'])
                op('dve', lambda e: e.tensor_tensor(out=T16[s_][:, 0:512], in0=t1, in1=t2, op=ALU.add),
                   r=[f'U32b_{s_}'], w=[f'T16_{s_}'])
                tb = 4 + s_
                for j in range(4):
                    op('pe', lambda e, j=j: e.transpose(out=psbf(tb)[:, j * 128:(j + 1) * 128],
                                                        in_=T16[s_][:, j * 128:(j + 1) * 128], identity=ident[:]),
                       r=[f'T16_{s_}', 'ident'], w=[f'PS{tb}'])
                op('act', lambda e: e.activation(out=SG[s_][:, 0:512], in_=psbf(tb)[:, 0:512], func=AF.Copy),
                   r=[f'PS{tb}'], w=[f'SG{s_}'])
                op('act', lambda e: e.dma_start(
                    out=dst[g * 4:(g + 1) * 4, :, tile * 128:(tile + 1) * 128].rearrange("h p t -> p h t"),
                    in_=SG[s_][:, 0:512].rearrange("p (h t) -> p h t", t=128)),
                   r=[f'SG{s_}'], w=[uniq()], dma=f'SG{s_}')

            def f(tile, bank):
                s_ = cnt['s'] % 2
                cnt['s'] += 1
                stage_a(tile, bank, s_)
                if pend:
                    stage_b(*pend.pop())
                pend.append((tile, bank, s_))

            def flush():
                if pend:
                    stage_b(*pend.pop())
            f.flush = flush
            return f

        phase(2)
        if P2ON:
            ropeC_d, ropeS_d, ropeCo_d, ropeSo_d = I('ropeC'), I('ropeS'), I('ropeCo'), I('ropeSo')
        else:
            ropeC_d = ropeS_d = ropeCo_d = ropeSo_d = None
        gemm(hTall, TALL, 512, 512, 'fm', epi_fm(lambda mc: gkTa[mc]))
        for g in range(2):
            gemm(hTall, TALL, 1024 + g * 512, 512, 'tm', epi_tm(gva, g * 512))
        gemm(hTall, TALL, 3072, 32, 'fm', epi_fm(lambda mc: (gafTa, gabTa)[mc], M=16), M=16)
        for g in range(2):
            gemm(hTall, TALL, 4128 + g * 512, 512, 'tm', epi_qk(g, wkn, 'wkn', ropeC_d, ropeS_d, dkT))
        for g in range(2):
            gemm(hTall, TALL, 5152 + g * 512, 512, 'tm', epi_dv(g))
        gemm(hTown, TOWN, 0, 512, 'fm', epi_fm(lambda mc: gqTo[mc]))
        gemm(hTown, TOWN, 512, 512, 'fm', epi_fm(lambda mc: gkTo[mc]))
        for g in range(2):
            gemm(hTown, TOWN, 1024 + g * 512, 512, 'tm', epi_tm(gvo, g * 512))
        for g in range(2):
            gemm(hTown, TOWN, 2048 + g * 512, 512, 'tm', epi_tm(gro, g * 512))
        gemm(hTown, TOWN, 3072, 32, 'fm', epi_fm(lambda mc: (gafTo, gabTo)[mc], M=16), M=16)
        for g in range(2):
            gemm(hTown, TOWN, 3104 + g * 512, 512, 'tm', epi_qk(g, wqn, 'wqn', ropeCo_d, ropeSo_d, dqT))
        sch.barrier()


        phase(3)
        P3ON = start_at <= 3 <= stop_after
        OF = av(0, [NTOWN, 1024], F32)
        gonw = av(20480, [1024], F32)
        MSK = {'f': av(22528, [128], F32), 'b': av(22784, [128], F32)}
        Sst = {'f': av(23040, [4, 128], F32), 'b': av(24064, [4, 128], F32)}
        Sbf = av(25088, [2, 4, 128])
        gkt_s = [[av(26112, [4, 128]), av(33536, [4, 128])], [av(35840, [4, 128]), av(36352, [4, 128])]]
        gqt_s = [[av(26624, [4, 128]), av(34048, [4, 128])], [av(36864, [4, 128]), av(37376, [4, 128])]]
        gvt_s = [[av(27136, [1024]), av(34560, [1024])], [av(37888, [1024]), av(38912, [1024])]]
        gcnt = [0, 0]
        WA = {'f': av(28160, [512]), 'b': av(28672, [512])}
        zA_s = {'f': [av(29184, [128]), av(35584, [128])], 'b': [av(29312, [128]), av(35712, [128])]}
        if P3ON:
            waf_d, wab_d, mskf_d, mskb_d, gonw_d = I('waf'), I('wab'), I('mskf'), I('mskb'), I('gonw')
        else:
            waf_d = wab_d = mskf_d = mskb_d = gonw_d = None
        op('sp', lambda e: e.dma_start(out=gonw[:], in_=gonw_d[:, :]), w=['gonw'], dma='gonw')
        for d, wd, md in (('f', waf_d, mskf_d), ('b', wab_d, mskb_d)):
            op('pool', lambda e, d=d, wd=wd: e.dma_start(out=WA[d][0:33, :], in_=wd[:, :]), w=[f'WA{d}'], dma=f'WA{d}')
            op('sp', lambda e, d=d, md=md: e.dma_start(out=MSK[d][:], in_=md[:, :]), w=[f'MSK{d}'], dma=f'MSK{d}')
            for z_ in range(2):
                op('dve', lambda e, d=d, z_=z_: e.memset(zA_s[d][z_][0:64, :], 0.0), w=[f'zA{d}{z_}'])
                op('dve', lambda e, d=d, z_=z_: e.memset(zA_s[d][z_][32:33, :], 1.0), w=[f'zA{d}{z_}'])
            op('dve', lambda e, d=d: e.memset(Sst[d][:], 0.0), w=[f'S{d}'])
        KZs = [av(29440, [2, 512]), av(39936, [2, 512])]
        for b_ in range(2):
            op('dve', lambda e, b_=b_: e.memset(KZs[b_].rearrange("p a b -> p (a b)"), 0.0), w=[f'ktok{b_}'])
        QZ = av(30464, [6, 512])
        op('dve', lambda e: e.memset(QZ[:].rearrange("p a b -> p (a b)"), 0.0), w=['QZ'])
        qz = [QZ[:, i, :].rearrange("p (f t) -> p f t", t=128) for i in range(6)]
        PT = SG[1][:, 0:1024].rearrange("p (h t) -> p h t", t=128)
        Gss = [stat[:, 40:48].rearrange("p (f c) -> p f c", c=2), stat[:, 48:56].rearrange("p (f c) -> p f c", c=2)]

        import os as _os
        CUT = int(_os.environ.get('P3CUT', '9'))

        def gla_tile(own, t, d, mi, outputs, bs=0):
            sl_ = gcnt[bs] % 2
            gcnt[bs] += 1
            gkt, gqt, gvt, zAd = gkt_s[bs][sl_], gqt_s[bs][sl_], gvt_s[bs][sl_], zA_s[d][sl_]
            kk, kq, kv, kz = f'gkt{bs}{sl_}', f'gqt{bs}{sl_}', f'gvt{bs}{sl_}', f'zA{d}{sl_}'
            B0 = 4 * bs
            KZ, Gs = KZs[bs], Gss[bs]
            ktT = T16[bs][:, 0:512].rearrange("p (f t) -> p f t", t=128)
            qtT = T16[bs][:, 512:1024].rearrange("p (f t) -> p f t", t=128)
            kP0, kP1 = f'PS{B0}', f'PS{B0 + 1}'
            gaT = (gafTo if d == 'f' else gabTo) if own else (gafTa if d == 'f' else gabTa)
            gkT_ = gkTo if own else gkTa
            gv_ = gvo if own else gva
            tc = slice(t * 128, (t + 1) * 128)
            op('sp', lambda e: e.dma_start(out=zAd[0:16, :], in_=gaT[:, tc]), w=[kz], dma=kz)
            op('sp', lambda e: e.dma_start(out=gkt[:], in_=gkT_[:, :, tc].rearrange("k p t -> p k t")), w=[kk], dma=kk)
            op('sp', lambda e: e.dma_start(out=gvt[:], in_=gv_[tc, :]), w=[kv], dma=kv)
            if outputs:
                op('sp', lambda e: e.dma_start(out=gqt[:], in_=gqTo[:, :, tc].rearrange("k p t -> p k t")),
                   w=[kq], dma=kq)
            op('pe', lambda e: e.matmul(PSB[B0][:, :], lhsT=zAd[0:33, :], rhs=WA[d][0:33, :], start=True, stop=True),
               r=[kz, f'WA{d}'], w=[kP0])
            op('act', lambda e: e.activation(out=U32[bs][:, 0:512], in_=PSB[B0][:, :], func=AF.Exp, scale=-1.0),
               r=[kP0], w=[f'e1{bs}'])
            op('act', lambda e: e.activation(out=T32[bs][:, 0:512], in_=U32[bs][:, 0:512], func=AF.Ln, bias=onec[:]),
               r=[f'e1{bs}', 'onec'], w=[f'L{bs}'])
            op('dve', lambda e: e.tensor_scalar(out=T32[bs][:, 512:1024], in0=T32[bs][:, 0:512], scalar1=mcols[:, mi:mi + 1],
                                                scalar2=None, op0=ALU.mult), r=[f'L{bs}', 'mcols'], w=[f'Lm{bs}'])
            for fc in range(4):
                op('pe', lambda e, fc=fc: e.matmul(PSB[B0][:, fc * 128:(fc + 1) * 128],
                                                   lhsT=T32[bs][:, 512 + fc * 128:512 + (fc + 1) * 128], rhs=MSK[d][:],
                                                   start=True, stop=True), r=[f'Lm{bs}', f'MSK{d}'], w=[kP0])
            op('act', lambda e: e.activation(out=U32[bs][:, 512:1024], in_=PSB[B0][:, :], func=AF.Exp, scale=1.0 / 16),
               r=[kP0], w=[f'Em{bs}'])
            g0 = 63 if d == 'f' else 0
            op('act', lambda e: e.activation(
                out=Gs, in_=PSB[B0][:, :].rearrange("p (f c x) -> p f c x", c=2, x=64)[:, :, :, g0],
                func=AF.Exp, scale=-1.0 / 16), r=[kP0], w=[f'Gs{bs}'])
            if outputs:
                op('act', lambda e: e.activation(out=U32[bs][:, 1024:1536], in_=PSB[B0][:, :], func=AF.Exp, scale=-1.0 / 16),
                   r=[kP0], w=[f'Ep{bs}'])
            op('dve', lambda e: e.tensor_tensor(out=T16[bs][:, 0:512], in0=gkt[:].rearrange("p f t -> p (f t)"),
                                                in1=U32[bs][:, 512:1024], op=ALU.mult), r=[kk, f'Em{bs}'], w=[f'ktT{bs}'])
            if outputs:
                op('dve', lambda e: e.scalar_tensor_tensor(
                    out=T16[bs][:, 512:1024], in0=gqt[:].rearrange("p f t -> p (f t)"), scalar=0.125,
                    in1=U32[bs][:, 1024:1536], op0=ALU.mult, op1=ALU.mult), r=[kq, f'Ep{bs}'], w=[f'qtT{bs}'])
                for p_ in range(2):
                    ps_ = slice(p_ * 64, p_ * 64 + 64)
                    op('pool', lambda e, p_=p_, ps_=ps_: e.tensor_copy(out=qz[4 + p_][ps_, :, :], in_=qtT[ps_, :, :]),
                       r=[f'qtT{bs}'], w=['QZ'])
                    for c in range(2):
                        op('pool', lambda e, c=c, p_=p_, ps_=ps_: e.tensor_copy(
                            out=qz[2 * c + p_][ps_, :, c * 64:(c + 1) * 64], in_=qtT[ps_, :, c * 64:(c + 1) * 64]),
                            r=[f'qtT{bs}'], w=['QZ'])
            for fc in range(4):
                op('pe', lambda e, fc=fc: e.transpose(out=psbf(B0 + 1)[:, fc * 128:(fc + 1) * 128], in_=ktT[:, fc, :],
                                                      identity=ident[:]), r=[f'ktT{bs}', 'ident'], w=[kP1])
            for c in range(2):
                cs = slice(c * 64, c * 64 + 64)
                op('dve', lambda e, c=c, cs=cs: e.tensor_scalar(out=KZ[cs, c, :], in0=psbf(B0 + 1)[cs, 0:512],
                                                                scalar1=mcols[cs, mi:mi + 1], scalar2=None, op0=ALU.mult),
                   r=[kP1, 'mcols'], w=[f'ktok{bs}'])
            P3V = int(_os.environ.get('P3V', '0'))
            if outputs and CUT >= 3:
                for h in (range(0, 8, 2) if P3V == 2 else range(8)):
                    fc, hb = h // 2, (h % 2) * 64
                    op('pe', lambda e, h=h, fc=fc, hb=hb: e.matmul(
                        PSB[4 + h // 4][:, (h % 4) * 128:(h % 4 + 1) * 128], lhsT=ktT[:, fc, :],
                        rhs=qz[4 + h % 2][:, fc, :], start=True, stop=True), r=[f'ktT{bs}', 'QZ'], w=[f'PS{4 + h // 4}'])
                for h in (range(0) if P3V in (1, 2) else range(8)):
                    op('dve', lambda e, h=h: e.tensor_tensor(
                        out=PT[:, h, :], in0=PSB[4 + h // 4][:, (h % 4) * 128:(h % 4 + 1) * 128], in1=MSK[d][:],
                        op=ALU.mult), r=[f'PS{4 + h // 4}', f'MSK{d}'], w=['PT'])
            corder = (0, 1) if d == 'f' else (1, 0)

            def chunk_step(c):
                for fc in range(4):
                    op('pe', lambda e, fc=fc, c=c: e.matmul(
                        PSB[B0 + 2 + fc // 2][:, (fc % 2) * 256:(fc % 2 + 1) * 256],
                        lhsT=KZ[:, c, fc * 128:(fc + 1) * 128],
                        rhs=gvt[:, fc * 256:(fc + 1) * 256], start=True, stop=True),
                        r=[f'ktok{bs}', kv], w=[f'PS{B0 + 2 + fc // 2}'])
                for bnk in range(2):
                    for half in range(2):
                        hs = slice(half * 64, half * 64 + 64)
                        op('dve', lambda e, bnk=bnk, half=half, hs=hs: e.tensor_tensor(
                            out=Sst[d][hs, 2 * bnk:2 * bnk + 2, :], in0=Sst[d][hs, 2 * bnk:2 * bnk + 2, :],
                            in1=PSB[B0 + 2 + bnk][hs, :].rearrange("p (f x) -> p f x", x=256)[:, :, half * 128:half * 128 + 128],
                            op=ALU.add), r=[f'PS{B0 + 2 + bnk}', f'S{d}'], w=[f'S{d}'])
                for fc in range(4):
                    op('dve', lambda e, fc=fc, c=c: e.tensor_scalar(
                        out=Sst[d][:, fc, :], in0=Sst[d][:, fc, :], scalar1=Gs[:, fc, c:c + 1], scalar2=None,
                        op0=ALU.mult), r=[f'S{d}', f'Gs{bs}'], w=[f'S{d}'])

            def snap(i):
                op('act', lambda e: e.activation(out=Sbf[:, i, :, :].rearrange("p f x -> p (f x)"),
                                                 in_=Sst[d][:].rearrange("p f x -> p (f x)"), func=AF.Copy),
                   r=[f'S{d}'], w=[f'Sbf{i}'])

            if outputs:
                snap(0)
            chunk_step(corder[0])
            if outputs:
                snap(1)
            chunk_step(corder[1])
            if outputs:
                for h in range(8):
                    fc = h // 2
                    ob_ = PSB[6 + h // 4][:, (h % 4) * 128:(h % 4 + 1) * 128]
                    op('pe', lambda e, h=h, ob_=ob_: e.matmul(ob_, lhsT=PT[:, h, :], rhs=gvt[:, h * 128:(h + 1) * 128],
                                                              start=True, stop=False), r=['PT', kv], w=[f'PS{6 + h // 4}'])
                    for ci, c in enumerate(corder):
                        op('pe', lambda e, h=h, fc=fc, c=c, ci=ci, ob_=ob_: e.matmul(
                            ob_, lhsT=qz[2 * c + h % 2][:, fc, :], rhs=Sbf[:, ci, fc, :], start=False, stop=(ci == 1)),
                            r=['QZ', f'Sbf{ci}'], w=[f'PS{6 + h // 4}'])
            if outputs and CUT >= 6:
                for bnk in range(2):
                    dst_ = OF[:, t, bnk * 512:(bnk + 1) * 512]
                    if d == 'f':
                        op('act', lambda e, bnk=bnk, dst_=dst_: e.activation(out=dst_, in_=PSB[6 + bnk][:, :], func=AF.Copy),
                           r=[f'PS{6 + bnk}'], w=[f'OF{t}'])
                    else:
                        op('dve', lambda e, bnk=bnk, dst_=dst_: e.tensor_tensor(out=dst_, in0=dst_, in1=PSB[6 + bnk][:, :],
                                                                                op=ALU.add),
                           r=[f'PS{6 + bnk}', f'OF{t}'], w=[f'OF{t}'])

        def gla_finish(t):
            tc = slice(t * 128, (t + 1) * 128)
            rt = XB[0][:, 2:4, :].rearrange("p a b -> p (a b)")
            yab = XB[0][:, 0:2, :].rearrange("p a b -> p (a b)")
            yat = XB[1][:, 0:2, :].rearrange("p a b -> p (a b)")
            op('sp', lambda e: e.dma_start(out=rt, in_=gro[tc, :]), w=['rt'], dma='rt')
            op('act', lambda e: e.activation(out=U32[1][:, 0:1024], in_=OF[:, t, :], func=AF.Square),
               r=[f'OF{t}'], w=['osq'])
            op('dve', lambda e: e.tensor_reduce(out=stat[:, 48:56],
                                                in_=U32[1][:, 0:1024].rearrange("p (h x) -> p h x", x=128),
                                                axis=AX.X, op=ALU.add), r=['osq'], w=['ost'])
            op('act', lambda e: e.activation(out=stat[:, 56:64], in_=stat[:, 48:56], func=AF.Sqrt, scale=1.0 / 128,
                                             bias=epsc[:]), r=['ost', 'epsc'], w=['osd'])
            op('dve', lambda e: e.reciprocal(out=stat[:, 48:56], in_=stat[:, 56:64]), r=['osd'], w=['ors'])
            op('act', lambda e: e.activation(out=T32[1][:, 0:1024], in_=rt, func=AF.Silu), r=['rt'], w=['sr'])
            for h in range(8):
                hs = slice(h * 128, (h + 1) * 128)
                op('dve', lambda e, h=h, hs=hs: e.scalar_tensor_tensor(
                    out=U32[1][:, 1024 + h * 128:1024 + (h + 1) * 128], in0=OF[:, t, hs], scalar=stat[:, 48 + h:49 + h],
                    in1=gonw[:, hs], op0=ALU.mult, op1=ALU.mult), r=[f'OF{t}', 'ors', 'gonw'], w=['yaf'])
            op('dve', lambda e: e.tensor_tensor(out=yab, in0=U32[1][:, 1024:2048], in1=T32[1][:, 0:1024], op=ALU.mult),
               r=['yaf', 'sr'], w=['yab'])
            for h in range(8):
                op('pe', lambda e, h=h: e.transpose(out=psbf(1)[:, h * 128:(h + 1) * 128], in_=yab[:, h * 128:(h + 1) * 128],
                                                    identity=ident[:]), r=['yab', 'ident'], w=['PS1'])
            op('act', lambda e: e.activation(out=yat, in_=psbf(1)[:, :], func=AF.Copy), r=['PS1'], w=['yat'])
            op('act', lambda e: e.dma_start(out=yaT[:, :, tc].rearrange("h p t -> p h t"),
                                           in_=yat.rearrange("p (h t) -> p h t", t=128)), r=['yat'], w=[uniq()], dma='yat')

        import os
        DBG = int(os.environ.get("P3DBG", "0"))
        if DBG == 1:
            gla_tile(False, 0, 'f', 0, False)
        elif DBG == 2:
            gla_tile(True, 0, 'f', 2 * NTALL, True)
        elif DBG == 3:
            gla_tile(True, 0, 'f', 2 * NTALL, True)
            gla_tile(True, 0, 'b', 2 * NTALL, True)
            gla_finish(0)
        else:
            border = [1, 0] + list(range(NTALL - 1, 1, -1))
            for i_ in range(NTALL):
                gla_tile(False, i_, 'f', i_, False, bs=0)
                gla_tile(False, border[i_], 'b', NTALL + border[i_], False, bs=1)
            for t in range(NTOWN):
                gla_tile(True, t, 'f', 2 * NTALL + t, True)
            for t in range(NTOWN - 1, -1, -1):
                gla_tile(True, t, 'b', 2 * NTALL + t, True)
                gla_finish(t)
        sch.barrier()

        phase(4)
        P4ON = start_at <= 4 <= stop_after
        KTh = [av(0, [4608]), av(4608, [4608])]
        V1h = [av(9216, [36, 129]), av(13860, [36, 129])]
        Qz = [av(18504, [2, 1280]), av(21064, [2, 1280])]
        od = av(23624, [10, 128], F32)
        ybb = av(26184, [1280])
        ybt = av(27464, [1280])
        donw = av(28744, [128], F32)
        lamv = av(29000, [4, 64], F32)
        epsd = sb("epsd", [128, 1], F32)
        if P4ON:
            donw_d, lamv_d = I('donw'), I('lamv')
        else:
            donw_d = lamv_d = None
        op('dve', lambda e: e.memset(epsd[:], EPS / (1.0 - LAM_INIT) ** 2), w=['epsd'])
        op('sp', lambda e: e.dma_start(out=donw, in_=donw_d[:, :]), w=['donw'], dma='donw')
        op('sp', lambda e: e.dma_start(out=lamv, in_=lamv_d[:, :, :]), w=['lamv'], dma='lamv')
        for s_ in range(2):
            op('dve', lambda e, s_=s_: e.memset(Qz[s_].rearrange("p m t -> p (m t)"), 0.0), w=[f'Qz{s_}'])
        for i_ in range(2):
            op('dve', lambda e, i_=i_: e.tensor_tensor(out=U32[1][:, i_ * 64:(i_ + 1) * 64], in0=lamv[:, 2 * i_, :],
                                                      in1=lamv[:, 2 * i_ + 1, :], op=ALU.mult), r=['lamv'], w=['lamp'])
            op('dve', lambda e, i_=i_: e.tensor_reduce(out=stat[:, 36 + i_:37 + i_], in_=U32[1][:, i_ * 64:(i_ + 1) * 64],
                                                      axis=AX.X, op=ALU.add), r=['lamp'], w=['lams'])
        op('act', lambda e: e.activation(out=stat[:, 60:62], in_=stat[:, 36:38], func=AF.Exp), r=['lams'], w=['lame'])
        op('dve', lambda e: e.tensor_tensor(out=stat[:, 38:39], in0=stat[:, 60:61], in1=stat[:, 61:62], op=ALU.subtract),
           r=['lame'], w=['lamd'])
        op('dve', lambda e: e.tensor_scalar(out=stat[:, 39:40], in0=stat[:, 38:39], scalar1=-1.0, scalar2=-LAM_INIT,
                                            op0=ALU.mult, op1=ALU.add), r=['lamd'], w=['neglam'])
        kts = [0, 1] + list(range(3, 35))
        for h in range(8):
            s_ = h % 2
            op('sp', lambda e, h=h, s_=s_: e.dma_start(out=KTh[s_], in_=dkT[h]), w=[f'KTh{s_}'], dma=f'KTh{s_}')
            op('sp', lambda e, h=h, s_=s_: e.dma_start(out=V1h[s_], in_=dv1[:, h, :].rearrange("(n p) d -> p n d", p=128)),
               w=[f'V1h{s_}'], dma=f'V1h{s_}')
            for m in range(2):
                ms = slice(m * 64, m * 64 + 64)
                op('sp', lambda e, h=h, s_=s_, m=m, ms=ms: e.dma_start(out=Qz[s_][ms, m, :], in_=dqT[h][ms, :]),
                   w=[f'Qz{s_}'], dma=f'Qz{s_}{m}')
            for (q0, nq) in ((0, 512), (512, 512), (1024, 256)):
                nqt = nq // 128
                for m in range(2):
                    def qk_exp(ki):
                        kt = kts[ki]
                        sbk = ki % 2
                        op('pe', lambda e, s_=s_, kt=kt, m=m, q0=q0, nq=nq, sbk=sbk: e.matmul(
                            PSB[sbk][:, 0:nq], lhsT=KTh[s_][:, kt * 128:(kt + 1) * 128], rhs=Qz[s_][:, m, q0:q0 + nq],
                            start=True, stop=True), r=[f'KTh{s_}', f'Qz{s_}'], w=[f'PS{sbk}'])
                        op('act', lambda e, nq=nq, sbk=sbk: e.activation(out=SG[sbk][:, 0:nq], in_=PSB[sbk][:, 0:nq],
                                                                        func=AF.Exp, scale=0.125),
                           r=[f'PS{sbk}'], w=[f'SG{sbk}'])

                    qk_exp(0)
                    for ki, kt in enumerate(kts):
                        sbk = ki % 2
                        if ki + 1 < len(kts):
                            qk_exp(ki + 1)
                        for qi in range(nqt):
                            op('pe', lambda e, s_=s_, kt=kt, qi=qi, sbk=sbk, ki=ki: e.matmul(
                                PSB[2 + qi][:, 0:129], lhsT=SG[sbk][:, qi * 128:(qi + 1) * 128], rhs=V1h[s_][:, kt, :],
                                start=(ki == 0), stop=(ki == len(kts) - 1)),
                                r=[f'SG{sbk}', f'V1h{s_}'], w=[f'PS{2 + qi}'])
                    for qi in range(nqt):
                        qt = q0 // 128 + qi
                        O_ = PSB[2 + qi]
                        op('dve', lambda e, qi=qi, O_=O_: e.reciprocal(out=stat[:, qi:qi + 1], in_=O_[:, 128:129]),
                           r=[f'PS{2 + qi}'], w=[f'rz{qi}'])
                        if m == 0:
                            op('dve', lambda e, qi=qi, qt=qt, O_=O_: e.tensor_scalar(
                                out=od[:, qt, :], in0=O_[:, 0:128], scalar1=stat[:, qi:qi + 1], scalar2=None, op0=ALU.mult),
                                r=[f'PS{2 + qi}', f'rz{qi}'], w=['od'])
                        else:
                            op('dve', lambda e, qi=qi: e.tensor_tensor(out=stat[:, 4 + qi:5 + qi], in0=stat[:, qi:qi + 1],
                                                                      in1=stat[:, 39:40], op=ALU.mult),
                               r=[f'rz{qi}', 'neglam'], w=[f'nrz{qi}'])
                            op('dve', lambda e, qi=qi, qt=qt, O_=O_: e.scalar_tensor_tensor(
                                out=od[:, qt, :], in0=O_[:, 0:128], scalar=stat[:, 4 + qi:5 + qi], in1=od[:, qt, :],
                                op0=ALU.mult, op1=ALU.add), r=[f'PS{2 + qi}', f'nrz{qi}', 'od'], w=['od'])
            odf = od.rearrange("p a b -> p (a b)")
            op('act', lambda e: e.activation(out=U32[0][:, 0:1280], in_=odf, func=AF.Square), r=['od'], w=['odsq'])
            op('dve', lambda e: e.tensor_reduce(out=stat[:, 8:18], in_=U32[0][:, 0:1280].rearrange("p (a b) -> p a b", b=128),
                                                axis=AX.X, op=ALU.add), r=['odsq'], w=['odst'])
            op('act', lambda e: e.activation(out=stat[:, 18:28], in_=stat[:, 8:18], func=AF.Sqrt,
                                             scale=1.0 / (128 * (1.0 - LAM_INIT) ** 2), bias=epsd[:]),
               r=['odst', 'epsd'], w=['odsd'])
            op('dve', lambda e: e.reciprocal(out=stat[:, 8:18], in_=stat[:, 18:28]), r=['odsd'], w=['odrs'])
            for qt in range(10):
                op('dve', lambda e, qt=qt: e.scalar_tensor_tensor(
                    out=ybb[:, qt * 128:(qt + 1) * 128], in0=od[:, qt, :], scalar=stat[:, 8 + qt:9 + qt], in1=donw,
                    op0=ALU.mult, op1=ALU.mult), r=['od', 'odrs', 'donw'], w=['ybb'])
            for qt in range(10):
                op('pe', lambda e, qt=qt: e.transpose(out=psbf(6 + qt // 8)[:, (qt % 8) * 128:(qt % 8 + 1) * 128],
                                                     in_=ybb[:, qt * 128:(qt + 1) * 128], identity=ident[:]),
                   r=['ybb', 'ident'], w=[f'PS{6 + qt // 8}'])
            op('act', lambda e: e.activation(out=ybt[:, 0:1024], in_=psbf(6)[:, :], func=AF.Copy), r=['PS6'], w=['ybt'])
            op('act', lambda e: e.activation(out=ybt[:, 1024:1280], in_=psbf(7)[:, 0:256], func=AF.Copy),
               r=['PS7'], w=['ybt'])
            op('act', lambda e, h=h: e.dma_start(out=ybT[h], in_=ybt), r=['ybt'], w=[uniq()], dma='ybt')
        sch.barrier()

        phase(5)
        P5ON = start_at <= 5 <= stop_after
        mergedT = av(0, [16, 1280])
        G1B = av(20480, [2048], F32)
        bg = av(24576, [32], F32)
        grow = av(24640, [2048], F32)
        ones_s = sb("ones_s", [128, 128], F32)
        if P5ON:
            w_gate_d, w_pg_d, w_pd_d, w_out_d, bgT_d, xown5_d = I('w_gate'), I('w_pg'), I('w_pd'), I('w_out'), I('bgT'), I('xown')
        else:
            w_gate_d = w_pg_d = w_pd_d = w_out_d = bgT_d = xown5_d = None
        phase(None)
        op('dve', lambda e: e.memset(ones_s[:], 1.0), w=['ones_s'])
        phase(5)

        def bcast_row(srcrow, dstB, key):
            op('sp', lambda e: e.dma_start(out=grow[0:1, :], in_=srcrow), w=['grow'], dma='grow')
            for c4 in range(4):
                op('pe', lambda e, c4=c4: e.matmul(PSB[4 + c4 % 2][:, :], lhsT=ones_s[0:1, :],
                                                   rhs=grow[0:1, c4 * 512:(c4 + 1) * 512], start=True, stop=True),
                   r=['ones_s', 'grow'], w=[f'PS{4 + c4 % 2}'])
                op('act', lambda e, c4=c4: e.activation(out=dstB[:, c4 * 512:(c4 + 1) * 512], in_=PSB[4 + c4 % 2][:, :],
                                                        func=AF.Copy), r=[f'PS{4 + c4 % 2}'], w=[key])

        bcast_row(modD[0:1, 2 * D:3 * D], G1B, 'G1B')
        op('sp', lambda e: e.dma_start(out=bg, in_=bgT_d[:, :]), w=['bg'], dma='bg')
        wc = 0
        for (q0, nq) in ((0, 512), (512, 512), (1024, 256)):
            op('sp', lambda e, q0=q0, nq=nq: e.dma_start(out=XB[0][:, :, 0:nq],
                                                         in_=hTown[:, :, q0:q0 + nq].rearrange("k p t -> p k t")),
               w=['XB0'], dma='XB0')
            op('sp', lambda e, q0=q0, nq=nq: e.dma_start(out=XB[1][:, 0:8, 0:nq],
                                                         in_=yaT[:, :, q0:q0 + nq].rearrange("k p t -> p k t")),
               w=['XB1a'], dma='XB1a')
            op('sp', lambda e, q0=q0, nq=nq: e.dma_start(out=XB[1][:, 8:16, 0:nq],
                                                         in_=ybT[:, :, q0:q0 + nq].rearrange("k p t -> p k t")),
               w=['XB1b'], dma='XB1b')
            for fo in range(16):
                ws = wc % 2
                wc += 1
                fs = slice(fo * 128, (fo + 1) * 128)
                op('pool', lambda e, ws=ws, fs=fs: e.dma_start(
                    out=WB[ws][:, :, 0:128], in_=w_gate_d[:, fs].rearrange("(k p) n -> p k n", p=128)),
                    w=[f'WB{ws}a'], dma=f'WB{ws}a')
                op('pool', lambda e, ws=ws, fo=fo: e.dma_start(
                    out=WB[ws][:, :, 128:256],
                    in_=w_gate_d[:, D + fo * 128:D + (fo + 1) * 128].rearrange("(k p) n -> p k n", p=128)),
                    w=[f'WB{ws}b'], dma=f'WB{ws}b')
                op('pool', lambda e, ws=ws, fs=fs: e.dma_start(
                    out=WB[ws][:, 0:8, 256:384], in_=w_pg_d[:, fs].rearrange("(k p) n -> p k n", p=128)),
                    w=[f'WB{ws}c'], dma=f'WB{ws}c')
                op('pool', lambda e, ws=ws, fs=fs: e.dma_start(
                    out=WB[ws][:, 0:8, 384:512], in_=w_pd_d[:, fs].rearrange("(k p) n -> p k n", p=128)),
                    w=[f'WB{ws}d'], dma=f'WB{ws}d')
                b0 = 4 * (fo % 2)
                for kc in range(KC):
                    op('pe', lambda e, ws=ws, kc=kc, nq=nq, b0=b0: e.matmul(
                        PSB[b0][:, 0:nq], lhsT=WB[ws][:, kc, 0:128], rhs=XB[0][:, kc, 0:nq],
                        start=(kc == 0), stop=(kc == KC - 1)), r=[f'WB{ws}a', 'XB0'], w=[f'PS{b0}'])
                for kc in range(KC):
                    op('pe', lambda e, ws=ws, kc=kc, nq=nq, b0=b0: e.matmul(
                        PSB[b0 + 1][:, 0:nq], lhsT=WB[ws][:, kc, 128:256], rhs=XB[0][:, kc, 0:nq],
                        start=(kc == 0), stop=(kc == KC - 1)), r=[f'WB{ws}b', 'XB0'], w=[f'PS{b0 + 1}'])
                for kc in range(8):
                    op('pe', lambda e, ws=ws, kc=kc, nq=nq, b0=b0: e.matmul(
                        PSB[b0 + 2][:, 0:nq], lhsT=WB[ws][:, kc, 256:384], rhs=XB[1][:, kc, 0:nq],
                        start=(kc == 0), stop=(kc == 7)), r=[f'WB{ws}c', 'XB1a'], w=[f'PS{b0 + 2}'])
                for kc in range(8):
                    op('pe', lambda e, ws=ws, kc=kc, nq=nq, b0=b0: e.matmul(
                        PSB[b0 + 3][:, 0:nq], lhsT=WB[ws][:, kc, 384:512], rhs=XB[1][:, 8 + kc, 0:nq],
                        start=(kc == 0), stop=(kc == 7)), r=[f'WB{ws}d', 'XB1b'], w=[f'PS{b0 + 3}'])
                for i_ in range(2):
                    op('act', lambda e, i_=i_, fo=fo, nq=nq, b0=b0: e.activation(
                        out=T32[i_][:, 0:nq], in_=PSB[b0 + i_][:, 0:nq], func=AF.Sigmoid,
                        bias=bg[:, 16 * i_ + fo:16 * i_ + fo + 1]), r=[f'PS{b0 + i_}', 'bg'], w=[f'T32_{i_}'])
                    op('dve', lambda e, i_=i_, nq=nq, b0=b0: e.tensor_tensor(
                        out=U32[i_][:, 0:nq], in0=T32[i_][:, 0:nq], in1=PSB[b0 + 2 + i_][:, 0:nq], op=ALU.mult),
                        r=[f'T32_{i_}', f'PS{b0 + 2 + i_}'], w=[f'U32_{i_}'])
                op('dve', lambda e, fo=fo, q0=q0, nq=nq: e.tensor_tensor(
                    out=mergedT[:, fo, q0:q0 + nq], in0=U32[0][:, 0:nq], in1=U32[1][:, 0:nq], op=ALU.add),
                    r=['U32_0', 'U32_1'], w=['mergedT'])
        def load_wout(g, ws):
            op('pool', lambda e: e.dma_start(
                out=WB[ws][:, :, :], in_=w_out_d[:, g * 512:(g + 1) * 512].rearrange("(k p) n -> p k n", p=128)),
                w=[f'WB{ws}a', f'WB{ws}b', f'WB{ws}c', f'WB{ws}d'], dma=f'WB{ws}a')

        load_wout(0, wc % 2)
        for g in range(4):
            ws = wc % 2
            wc += 1
            if g + 1 < 4:
                load_wout(g + 1, wc % 2)
            for tile in range(NTOWN):
                bank = tile % 2
                ts_ = slice(tile * 128, (tile + 1) * 128)
                for kc in range(KC):
                    op('pe', lambda e, ws=ws, kc=kc, bank=bank, ts_=ts_: e.matmul(
                        PSB[bank][:, :], lhsT=mergedT[:, kc, ts_], rhs=WB[ws][:, kc, :],
                        start=(kc == 0), stop=(kc == KC - 1)), r=[f'WB{ws}a', 'mergedT'], w=[f'PS{bank}'])
                op('sp', lambda e, bank=bank, ts_=ts_, g=g: e.dma_start(out=T32[bank][:, 0:512],
                                                                        in_=xown5_d[ts_, g * 512:(g + 1) * 512]),
                   w=[f'T32_{bank}'], dma=f'T32_{bank}')
                op('dve', lambda e, bank=bank, g=g: e.tensor_tensor(
                    out=U32[bank][:, 0:512], in0=PSB[bank][:, :], in1=G1B[:, g * 512:(g + 1) * 512], op=ALU.mult),
                    r=[f'PS{bank}', 'G1B'], w=[f'U32_{bank}'])
                op('dve', lambda e, bank=bank: e.tensor_tensor(
                    out=U32[bank][:, 512:1024], in0=U32[bank][:, 0:512], in1=T32[bank][:, 0:512], op=ALU.add),
                    r=[f'U32_{bank}', f'T32_{bank}'], w=[f'U32_{bank}'])
                op('pool', lambda e, bank=bank, ts_=ts_, g=g: e.dma_start(out=xnewD[ts_, g * 512:(g + 1) * 512],
                                                                        in_=U32[bank][:, 512:1024]),
                   r=[f'U32_{bank}'], w=[uniq()], dma=f'U32_{bank}')
        sch.barrier()

        phase(6)
        P6ON = start_at <= 6 <= stop_after
        h2T = av(0, [16, 1280])
        actT = av(20480, [44, 512])
        G2B = av(43008, [2048], F32)
        cw = av(47104, [88, 3], F32)
        cb = av(47632, [88], F32)
        if P6ON:
            w_up_d, w_down_d, cwT_d, cbT_d = I('w_up'), I('w_down'), I('cwT'), I('cbT')
        else:
            w_up_d = w_down_d = cwT_d = cbT_d = None
        bcast_row(modD[0:1, 5 * D:6 * D], G2B, 'G2B')
        op('sp', lambda e: e.dma_start(out=cw, in_=cwT_d[:, :, :]), w=['cw'], dma='cw')
        op('sp', lambda e: e.dma_start(out=cb, in_=cbT_d[:, :]), w=['cb'], dma='cb')
        k1(xnewD, NTOWN, None, lambda t: A2, lambda t: modT[:, 0, 3, :], 'h2T', dst_sb=h2T)
        for ci_, col in enumerate((127, 1152)):
            op('dve', lambda e, ci_=ci_, col=col: e.tensor_scalar(
                out=h2T[:, :, col:col + 1], in0=h2T[:, :, col:col + 1], scalar1=edge[:, ci_:ci_ + 1], scalar2=None,
                op0=ALU.mult), r=['h2T', 'edge'], w=['h2T'])
        if _os.environ.get('P6TAP'):
            op('sp', lambda e: e.dma_start(out=hTown.rearrange("k p t -> p k t"), in_=h2T), r=['h2T'], w=[uniq()], dma='h2Ttap')
        wc = 0
        for hf in range(2):
            c0 = 128 + hf * 512
            for j in range(NJ):
                ws = wc % 2
                wc += 1
                op('pool', lambda e, ws=ws, j=j: e.dma_start(
                    out=WB[ws][:, :, 0:128], in_=w_up_d[:, j * 128:(j + 1) * 128].rearrange("(k p) n -> p k n", p=128)),
                    w=[f'WB{ws}a'], dma=f'WB{ws}a')
                op('pool', lambda e, ws=ws, j=j: e.dma_start(
                    out=WB[ws][:, :, 128:256],
                    in_=w_up_d[:, DFF + j * 128:DFF + (j + 1) * 128].rearrange("(k p) n -> p k n", p=128)),
                    w=[f'WB{ws}b'], dma=f'WB{ws}b')
                b0 = 4 * (j % 2)
                for i_ in range(2):
                    wkey = f'WB{ws}' + 'ab'[i_]
                    for (bo, ca, cn) in ((0, c0 - 1, 512), (1, c0 + 511, 2)):
                        for kc in range(KC):
                            op('pe', lambda e, ws=ws, kc=kc, i_=i_, bo=bo, ca=ca, cn=cn, b0=b0: e.matmul(
                                PSB[b0 + 2 * i_ + bo][:, 0:cn], lhsT=WB[ws][:, kc, i_ * 128:(i_ + 1) * 128],
                                rhs=h2T[:, kc, ca:ca + cn], start=(kc == 0), stop=(kc == KC - 1)),
                                r=[wkey, 'h2T'], w=[f'PS{b0 + 2 * i_ + bo}'])
                    op('act', lambda e, i_=i_, b0=b0: e.activation(out=T32[i_][:, 0:512], in_=PSB[b0 + 2 * i_][:, 0:512],
                                                                   func=AF.Copy), r=[f'PS{b0 + 2 * i_}'], w=[f'T32_{i_}'])
                    op('act', lambda e, i_=i_, b0=b0: e.activation(out=T32[i_][:, 512:514], in_=PSB[b0 + 2 * i_ + 1][:, 0:2],
                                                                   func=AF.Copy), r=[f'PS{b0 + 2 * i_ + 1}'], w=[f'T32_{i_}'])
                    jj = i_ * NJ + j
                    op('dve', lambda e, i_=i_, jj=jj: e.tensor_scalar(
                        out=U32[i_][:, 0:512], in0=T32[i_][:, 1:513], scalar1=cw[:, jj, 1:2], scalar2=None, op0=ALU.mult),
                        r=[f'T32_{i_}', 'cw'], w=[f'U32_{i_}'])
                    for tap, off in ((0, 0), (2, 2)):
                        op('dve', lambda e, i_=i_, jj=jj, tap=tap, off=off: e.scalar_tensor_tensor(
                            out=U32[i_][:, 0:512], in0=T32[i_][:, off:off + 512], scalar=cw[:, jj, tap:tap + 1],
                            in1=U32[i_][:, 0:512], op0=ALU.mult, op1=ALU.add),
                            r=[f'T32_{i_}', 'cw', f'U32_{i_}'], w=[f'U32_{i_}'])
                op('act', lambda e, j=j: e.activation(out=U32[0][:, 512:1024], in_=U32[0][:, 0:512], func=AF.Silu,
                                                      bias=cb[:, j:j + 1]), r=['U32_0', 'cb'], w=['sg'])
                op('dve', lambda e, j=j: e.scalar_tensor_tensor(
                    out=actT[:, j, :], in0=U32[1][:, 0:512], scalar=cb[:, NJ + j:NJ + j + 1], in1=U32[0][:, 512:1024],
                    op0=ALU.add, op1=ALU.mult), r=['U32_1', 'sg', 'cb'], w=['actT'])
            if _os.environ.get('P6TAP') and hf == 1:
                dbgA = dscr('dbgA', [NJ, 128, 512], BF16)
                op('sp', lambda e: e.dma_start(out=dbgA.rearrange("j p t -> p j t"), in_=actT), r=['actT'], w=[uniq()], dma='h2Ttap')
            for cg4 in range(4):
                bs = 4 * (cg4 % 2)
                for cgi in range(4):
                    cg = cg4 * 4 + cgi
                    xs = wc % 2
                    wc += 1
                    WD = XB[xs][:].rearrange("p a b -> p (a b)")[:, 0:NJ * 128].rearrange("p (j n) -> p j n", n=128)
                    op('pool', lambda e, WD=WD, cg=cg: e.dma_start(
                        out=WD, in_=w_down_d[:, cg * 128:(cg + 1) * 128].rearrange("(j p) n -> p j n", p=128)),
                        w=[f'XB{xs}'], dma=f'XB{xs}')
                    for ti in range(4):
                        for j in range(NJ):
                            op('pe', lambda e, WD=WD, ti=ti, j=j, bs=bs, cgi=cgi: e.matmul(
                                PSB[bs + ti][:, cgi * 128:(cgi + 1) * 128], lhsT=actT[:, j, ti * 128:(ti + 1) * 128],
                                rhs=WD[:, j, :], start=(j == 0), stop=(j == NJ - 1)),
                                r=[f'XB{xs}', 'actT'], w=[f'PS{bs + ti}'])
                for ti in range(4):
                    et = 1 + hf * 4 + ti
                    yr = slice((hf * 4 + ti) * 128, (hf * 4 + ti + 1) * 128)
                    gs = slice(cg4 * 512, (cg4 + 1) * 512)
                    sl = ti % 2
                    op('sp', lambda e, et=et, gs=gs, sl=sl: e.dma_start(out=T32[sl][:, 1024:1536],
                                                                        in_=xnewD[et * 128:(et + 1) * 128, gs]),
                       w=[f'T32_{sl}'], dma=f'T32_{sl}x')
                    op('dve', lambda e, ti=ti, bs=bs, gs=gs, sl=sl: e.tensor_tensor(
                        out=U32[sl][:, 1024:1536], in0=PSB[bs + ti][:, :], in1=G2B[:, gs], op=ALU.mult),
                        r=[f'PS{bs + ti}', 'G2B'], w=[f'U32_{sl}'])
                    op('dve', lambda e, sl=sl: e.tensor_tensor(
                        out=U32[sl][:, 1536:2048], in0=U32[sl][:, 1024:1536], in1=T32[sl][:, 1024:1536], op=ALU.add),
                        r=[f'U32_{sl}', f'T32_{sl}'], w=[f'U32_{sl}'])
                    op('act', lambda e, yr=yr, gs=gs, sl=sl: e.dma_start(out=y[yr, gs], in_=U32[sl][:, 1536:2048]),
                       r=[f'U32_{sl}'], w=[uniq()], dma=f'U32_{sl}z')
        sch.barrier()

        phase(None)
        if stop_after <= 5:
            op('dve', lambda e: e.memset(T32[0][:], 0.0), w=['T32_0'])
            op('sp', lambda e: e.dma_start(out=y[0:128, :], in_=T32[0][:]), r=['T32_0'], w=[uniq()], dma='T32_0')
        sch.barrier()
        sch.emit()
    return nc, list(declared)


def host_inputs(inputs, core):
    b, tq = core // 4, core % 4
    f = lambda a: np.ascontiguousarray(np.asarray(a, dtype=np.float32))
    x = f(inputs["x"])[b]
    ctx = f(inputs["ctx"])[b]
    pad = np.zeros((128, D), np.float32)
    xpad = np.concatenate([pad, x, pad], 0)
    m = {}
    m["xall"] = np.concatenate([ctx, xpad], 0)
    m["xown"] = xpad[tq * 1024: tq * 1024 + TOWN]
    c2 = np.stack([f(inputs["c"])[b], f(inputs["c_ctx"])], -1)
    m["cT"] = np.ascontiguousarray(c2.reshape(KC, 128, 2).transpose(1, 0, 2))
    m["w_ada"] = f(inputs["w_ada"])[0]
    m["b_ada2"] = np.ascontiguousarray(np.broadcast_to(f(inputs["b_ada"])[0][None], (2, 6 * D)))
    m["nw1T"] = np.ascontiguousarray(f(inputs["norm1_w"])[0].reshape(KC, 128).T)
    m["nw2T"] = np.ascontiguousarray(f(inputs["norm2_w"])[0].reshape(KC, 128).T)
    m["ident"] = np.eye(128, dtype=np.float32)
    m["w_in"] = f(inputs["w_in"])[0]
    bc = lambda v, rep: np.ascontiguousarray(np.broadcast_to(np.tile(f(v), rep)[None], (128, f(v).size * rep)))
    m["qnw"] = bc(inputs["diff_qnorm_w"][0], 8)
    m["knw"] = bc(inputs["diff_knorm_w"][0], 8)
    inv = (10000.0 ** (-np.arange(16, dtype=np.float32) / 16)).astype(np.float32)

    def rope_tab(tok):
        valid = (tok >= 0) & (tok < S)
        tt = np.where(valid, tok, 0)
        ar = ((tt // 64).astype(np.float32)[:, None] * inv).astype(np.float32)
        ac = ((tt % 64).astype(np.float32)[:, None] * inv).astype(np.float32)
        cr, sr, cc, sc = np.cos(ar), np.sin(ar), np.cos(ac), np.sin(ac)
        C = np.concatenate([cr, cr, cc, cc], 1)
        Sg = np.concatenate([-sr, sr, -sc, sc], 1)
        C = np.where(valid[:, None], C, 1.0).astype(np.float32)
        Sg = np.where(valid[:, None], Sg, 0.0).astype(np.float32)
        return np.ascontiguousarray(np.tile(C, (1, 8))), np.ascontiguousarray(np.tile(Sg, (1, 8)))
    def aug(wu, bu):
        a = np.zeros((33, 512), np.float32)
        a[0:16] = f(wu)[0]
        a[32] = f(bu)[0]
        return a
    m["waf"] = aug(inputs["w_a_up_f"], inputs["b_a_f"])
    m["wab"] = aug(inputs["w_a_up_b"], inputs["b_a_b"])
    jj, ii = np.meshgrid(np.arange(128), np.arange(128), indexing="ij")
    same = (jj // 64) == (ii // 64)
    m["mskf"] = (same & (jj <= ii)).astype(np.float32)
    m["mskb"] = (same & (jj >= ii)).astype(np.float32)
    m["gonw"] = bc(inputs["gla_onorm_w"][0], 8)
    e0 = 2 + 8 * tq
    real = lambda g_: 3 <= g_ <= 34
    mc = np.zeros((3 * NTALL,), np.float32)
    for t_ in range(NTALL):
        mc[t_] = 1.0 if (t_ < 2 or (real(t_) and t_ < e0)) else 0.0
        mc[NTALL + t_] = 1.0 if (t_ < 2 or (real(t_) and t_ > e0 + 9)) else 0.0
    for e_ in range(NTOWN):
        mc[2 * NTALL + e_] = 1.0 if real(e0 + e_) else 0.0
    m["mcols"] = np.ascontiguousarray(np.broadcast_to(mc[None], (128, 3 * NTALL)))
    m["edge"] = np.ascontiguousarray(np.broadcast_to(
        np.array([0.0 if tq == 0 else 1.0, 0.0 if tq == 3 else 1.0], np.float32)[None], (128, 2)))
    m["donw"] = bc(inputs["diff_onorm_w"][0], 1)
    lv = np.stack([f(inputs[k_])[0] for k_ in ("lambda_q1", "lambda_k1", "lambda_q2", "lambda_k2")], 0)
    m["lamv"] = np.ascontiguousarray(np.broadcast_to(lv[None], (128, 4, 64)))
    m["w_pg"] = f(inputs["w_proj_gla"])[0]
    m["w_pd"] = f(inputs["w_proj_diff"])[0]
    m["w_gate"] = f(inputs["w_gate"])[0]
    m["bgT"] = np.ascontiguousarray(f(inputs["b_gate"])[0].reshape(32, 128).T)
    m["w_out"] = f(inputs["w_out"])[0]
    m["w_up"] = f(inputs["w_up"])[0]
    m["w_down"] = f(inputs["w_down"])[0]
    m["cwT"] = np.ascontiguousarray(f(inputs["conv_w"])[0].reshape(3, 88, 128).transpose(2, 1, 0))
    m["cbT"] = np.ascontiguousarray(f(inputs["conv_b"])[0].reshape(88, 128).T)
    pall = np.arange(TALL) - 384
    pall[:256] = -1
    m["ropeC"], m["ropeS"] = rope_tab(pall)
    m["ropeCo"], m["ropeSo"] = rope_tab(np.arange(TOWN) + tq * 1024 - 128)
    return m


_NC_CACHE = {}


def kernel(**inputs):
    if "nc" not in _NC_CACHE:
        _NC_CACHE["nc"] = build()
    nc, decl = _NC_CACHE["nc"]
    maps = []
    for core in range(8):
        m = host_inputs(inputs, core)
        maps.append({k: m[k] for k in decl})
    res = run_bass_kernel_spmd(nc, maps, core_ids=list(range(8)))
    out = np.zeros((2, S, D), np.float32)
    for core in range(8):
        b, tq = core // 4, core % 4
        out[b, tq * 1024:(tq + 1) * 1024, :] = np.asarray(res.results[core]["y"], dtype=np.float32)
    return out
```
